# Optimizing a Trainium2 kernel written in Bass

```python
import numpy as np
import jax
import jax.numpy as jnp
from jax import lax

D_MODEL = 2048
BATCH = 2
SEQ = 8192
DEPTH = 2

GRID_W = 64
CTX_LEN = 256
N_MIXERS = 4
D_GROUP = D_MODEL // N_MIXERS
HEAD_DIM = 128
GLA_HEADS = D_GROUP // HEAD_DIM
GLA_LOWRANK = 16
GLA_GATE_NORM = 16.0
GLA_CHUNK = 64
MLSTM_HEADS = D_GROUP // HEAD_DIM
MLSTM_CHUNK = 64
CM_GROUPS = D_GROUP // HEAD_DIM
CM_CHUNK = 128
CONV_W = 3
D_FF = 5632
EPS = 1e-6

GLA_COLS = 4 * D_GROUP + 2 * GLA_LOWRANK
MLSTM_COLS = 4 * D_GROUP + 4 * MLSTM_HEADS
SC_COLS = 3 * D_GROUP
CM_COLS = 2 * D_GROUP
IN_COLS = GLA_COLS + MLSTM_COLS + SC_COLS + CM_COLS

kernel_name = 'hybrid_parallel_gla_mlstm_conv_chunkmlp_adaln'


def rmsnorm(x, g):
    xf = x.astype(jnp.float32)
    y = xf * lax.rsqrt(jnp.mean(xf * xf, axis=-1, keepdims=True) + EPS)
    return (y * g.astype(jnp.float32)).astype(x.dtype)


def head_rmsnorm(x, g, n_heads):
    b, l, w = x.shape
    xh = x.reshape(b, l, n_heads, w // n_heads)
    return rmsnorm(xh, g.reshape(n_heads, w // n_heads)).reshape(b, l, w)


def split_cols(p, sizes):
    idx = [int(i) for i in np.cumsum(sizes)[:-1]]
    return jnp.split(p, idx, axis=-1)


def to_heads(t, n_heads):
    b, l, w = t.shape
    return t.astype(jnp.float32).reshape(b, l, n_heads, w // n_heads).transpose(0, 2, 1, 3)


def from_heads(t):
    b, h, l, d = t.shape
    return t.transpose(0, 2, 1, 3).reshape(b, l, h * d)


def dwconv3(x, w, on_grid):
    b, l, ch = x.shape
    if on_grid:
        rows = l // GRID_W
        x = x.reshape(b, rows, GRID_W, ch)
    ax = x.ndim - 2
    n = x.shape[ax]
    pad = [(0, 0)] * x.ndim
    pad[ax] = (CONV_W // 2, CONV_W // 2)
    xp = jnp.pad(x, pad)
    y = sum(w[j] * lax.slice_in_dim(xp, j, j + n, axis=ax) for j in range(CONV_W))
    return y.reshape(b, l, ch)


def _chunks(t, size):
    b, h, l = t.shape[:3]
    return jnp.moveaxis(t.reshape(b, h, l // size, size, *t.shape[3:]), 2, 0)


def _unchunks(t):
    t = jnp.moveaxis(t, 0, 2)
    return t.reshape(t.shape[0], t.shape[1], -1, t.shape[-1])


def gla_scan(q, k, v, log_a, state):
    size = GLA_CHUNK
    xs = (_chunks(q, size), _chunks(k, size), _chunks(v, size), _chunks(log_a, size))
    mask = jnp.tril(jnp.ones((size, size), dtype=bool))

    def step(s_mat, inp):
        qc, kc, vc, ac = inp
        b_cum = jnp.cumsum(ac, axis=-2)
        b_last = b_cum[..., -1:, :]
        q_in = qc * jnp.exp(b_cum)
        q_rel = qc * jnp.exp(b_cum - b_last)
        k_rel = kc * jnp.exp(b_last - b_cum)
        att = jnp.where(mask, jnp.einsum('bhtd,bhsd->bhts', q_rel, k_rel), 0.0)
        o = jnp.einsum('bhtd,bhdv->bhtv', q_in, s_mat) + jnp.einsum('bhts,bhsv->bhtv', att, vc)
        s_new = jnp.swapaxes(jnp.exp(b_last), -1, -2) * s_mat + jnp.einsum('bhsd,bhsv->bhdv', k_rel, vc)
        return s_new, o

    state, o = lax.scan(step, state, xs)
    return _unchunks(o), state


def mlstm_scan(q, k, v, log_i, log_f, state):
    size = MLSTM_CHUNK
    xs = (_chunks(q, size), _chunks(k, size), _chunks(v, size), _chunks(log_i, size), _chunks(log_f, size))
    mask = jnp.tril(jnp.ones((size, size), dtype=bool))

    def step(carry, inp):
        c_mat, n_vec, m = carry
        qc, kc, vc, ic, fc = inp
        f_cum = jnp.cumsum(fc, axis=-1)
        d_log = jnp.where(mask, f_cum[..., :, None] - f_cum[..., None, :] + ic[..., None, :], -jnp.inf)
        inter = f_cum + m[..., None]
        m_t = jnp.maximum(inter, jnp.max(d_log, axis=-1))
        w_inter = jnp.exp(inter - m_t)
        s = jnp.einsum('bhtd,bhsd->bhts', qc, kc) * jnp.exp(d_log - m_t[..., None])
        num = w_inter[..., None] * jnp.einsum('bhtd,bhdv->bhtv', qc, c_mat) + jnp.einsum('bhts,bhsv->bhtv', s, vc)
        den = w_inter * jnp.einsum('bhtd,bhd->bht', qc, n_vec) + jnp.sum(s, axis=-1)
        h = num / jnp.maximum(jnp.abs(den), jnp.exp(-m_t))[..., None]
        m_new = m_t[..., -1]
        decay = jnp.exp(f_cum[..., -1] + m - m_new)
        w_k = jnp.exp(f_cum[..., -1:] - f_cum + ic - m_new[..., None])
        c_new = decay[..., None, None] * c_mat + jnp.einsum('bhs,bhsd,bhsv->bhdv', w_k, kc, vc)
        n_new = decay[..., None] * n_vec + jnp.einsum('bhs,bhsd->bhd', w_k, kc)
        return (c_new, n_new, m_new), h

    state, h = lax.scan(step, state, xs)
    return _unchunks(h), state


def bidir_scan(scan_fn, ctx_fwd, lat_fwd, ctx_bwd, lat_bwd, init):
    flip = lambda t: jnp.flip(t, axis=2)
    oc_f, s_f = scan_fn(*ctx_fwd, init)
    ol_f, _ = scan_fn(*lat_fwd, s_f)
    oc_b, s_b = scan_fn(*[flip(t) for t in ctx_bwd], init)
    ol_b, _ = scan_fn(*[flip(t) for t in lat_bwd], s_b)
    return oc_f + flip(oc_b), ol_f + flip(ol_b)


def gla_prep(p, w_a2, b_a):
    q, k, v, g, z = split_cols(p, [D_GROUP] * 4 + [2 * GLA_LOWRANK])
    b, l, _ = z.shape
    z = z.astype(jnp.float32).reshape(b, l, 2, GLA_LOWRANK)
    log_a = jax.nn.log_sigmoid(jnp.einsum('bldr,drk->bldk', z, w_a2.astype(jnp.float32))
                               + b_a.astype(jnp.float32)) / GLA_GATE_NORM
    q = to_heads(q, GLA_HEADS) * HEAD_DIM ** -0.5
    k = to_heads(k, GLA_HEADS)
    v = to_heads(v, GLA_HEADS)
    fwd = (q, k, v, to_heads(log_a[:, :, 0], GLA_HEADS))
    bwd = (q, k, v, to_heads(log_a[:, :, 1], GLA_HEADS))
    return fwd, bwd, g


def gla_mixer(pc, pl, w_a2, b_a, g_norm, need_ctx):
    cf, cb, gc = gla_prep(pc, w_a2, b_a)
    lf, lb, gl = gla_prep(pl, w_a2, b_a)
    init = jnp.zeros(cf[0].shape[:2] + (HEAD_DIM, HEAD_DIM), jnp.float32)
    oc, ol = bidir_scan(gla_scan, cf, lf, cb, lb, init)

    def out(o, g):
        return (head_rmsnorm(from_heads(o), g_norm, GLA_HEADS) * jax.nn.silu(g.astype(jnp.float32))).astype(g.dtype)

    return (out(oc, gc) if need_ctx else None), out(ol, gl)


def mlstm_prep(p, conv_w, gate_b, on_grid):
    qk, v, o, gates = split_cols(p, [2 * D_GROUP, D_GROUP, D_GROUP, 4 * MLSTM_HEADS])
    q, k = jnp.split(dwconv3(qk, conv_w, on_grid), 2, axis=-1)
    q = to_heads(q, MLSTM_HEADS)
    k = to_heads(k, MLSTM_HEADS) * HEAD_DIM ** -0.5
    v = to_heads(v, MLSTM_HEADS)
    b, l, _ = gates.shape
    gates = gates.astype(jnp.float32).reshape(b, l, 2, 2, MLSTM_HEADS) + gate_b.astype(jnp.float32)
    gates = gates.transpose(2, 3, 0, 4, 1)
    log_i = gates[:, 0]
    log_f = jax.nn.log_sigmoid(gates[:, 1])
    return (q, k, v, log_i[0], log_f[0]), (q, k, v, log_i[1], log_f[1]), o


def mlstm_mixer(pc, pl, conv_w, gate_b, g_norm, need_ctx):
    cf, cb, oc_gate = mlstm_prep(pc, conv_w, gate_b, False)
    lf, lb, ol_gate = mlstm_prep(pl, conv_w, gate_b, True)
    bh = cf[0].shape[:2]
    init = (jnp.zeros(bh + (HEAD_DIM, HEAD_DIM), jnp.float32),
            jnp.zeros(bh + (HEAD_DIM,), jnp.float32),
            jnp.zeros(bh, jnp.float32))
    hc, hl = bidir_scan(mlstm_scan, cf, lf, cb, lb, init)

    def out(h, o):
        return (head_rmsnorm(from_heads(h), g_norm, MLSTM_HEADS) * jax.nn.sigmoid(o.astype(jnp.float32))).astype(o.dtype)

    return (out(hc, oc_gate) if need_ctx else None), out(hl, ol_gate)


def short_conv(p, w, on_grid):
    bg, cg, h = split_cols(p, [D_GROUP] * 3)
    return bg * dwconv3(cg * h, w, on_grid)


def chunk_mlp(p, w_s, b_s, g_v):
    u, v = jnp.split(p, 2, axis=-1)
    u = jax.nn.gelu(u)
    v = rmsnorm(jax.nn.gelu(v), g_v)
    b, l, _ = v.shape
    vc = v.reshape(b, l // CM_CHUNK, CM_CHUNK, CM_GROUPS, D_GROUP // CM_GROUPS)
    sv = jnp.einsum('gts,bnsgc->bntgc', w_s, vc) + b_s.T[:, :, None]
    return u * sv.reshape(b, l, D_GROUP)


def conv_ffn(h, w_up, w_conv, w_down, on_grid):
    a, v = jnp.split(h @ w_up, 2, axis=-1)
    return (jax.nn.silu(dwconv3(a, w_conv, on_grid)) * v) @ w_down


def modulation(cond, w, b):
    m = jax.nn.silu(cond) @ w + b
    m = m.reshape(m.shape[:-1] + (1, m.shape[-1]))
    return jnp.split(m, 6, axis=-1)


def setup_inputs(seed: int = 0) -> dict:
    key = jax.random.key(seed)
    ks = jax.random.split(key, 26)
    nrm = lambda k, shape, scale: jax.random.normal(k, shape, jnp.float32) * scale
    base_if = jnp.stack([jnp.zeros((MLSTM_HEADS,), jnp.float32),
                         jnp.linspace(3.0, 6.0, MLSTM_HEADS, dtype=jnp.float32)])
    return {
        'x': nrm(ks[0], (BATCH, SEQ, D_MODEL), 1.0),
        'c': nrm(ks[1], (BATCH, D_MODEL), 1.0),
        'ctx': nrm(ks[2], (BATCH, CTX_LEN, D_MODEL), 1.0),
        'c_ctx': nrm(ks[3], (D_MODEL,), 1.0),
        'w_mod': nrm(ks[4], (DEPTH, D_MODEL, 6 * D_MODEL), 0.5 * D_MODEL ** -0.5),
        'b_mod': nrm(ks[5], (DEPTH, 6 * D_MODEL), 0.01),
        'norm1': 1.0 + nrm(ks[6], (DEPTH, D_MODEL), 0.1),
        'norm2': 1.0 + nrm(ks[7], (DEPTH, D_MODEL), 0.1),
        'w_in': nrm(ks[8], (DEPTH, D_MODEL, IN_COLS), D_MODEL ** -0.5),
        'w_out': nrm(ks[9], (DEPTH, D_MODEL, D_MODEL), D_MODEL ** -0.5),
        'gla_wa2': nrm(ks[10], (DEPTH, 2, GLA_LOWRANK, D_GROUP), GLA_LOWRANK ** -0.5),
        'gla_ba': nrm(ks[11], (DEPTH, 2, D_GROUP), 0.1),
        'gla_norm': 1.0 + nrm(ks[12], (DEPTH, D_GROUP), 0.1),
        'mlstm_conv': nrm(ks[13], (DEPTH, CONV_W, 2 * D_GROUP), CONV_W ** -0.5),
        'mlstm_gate_b': base_if + nrm(ks[14], (DEPTH, 2, 2, MLSTM_HEADS), 0.1),
        'mlstm_norm': 1.0 + nrm(ks[15], (DEPTH, D_GROUP), 0.1),
        'sc_conv': nrm(ks[16], (DEPTH, CONV_W, D_GROUP), CONV_W ** -0.5),
        'cm_ws': nrm(ks[17], (DEPTH, CM_GROUPS, CM_CHUNK, CM_CHUNK), CM_CHUNK ** -0.5),
        'cm_bs': 1.0 + nrm(ks[18], (DEPTH, CM_GROUPS, CM_CHUNK), 0.1),
        'cm_norm': 1.0 + nrm(ks[19], (DEPTH, D_GROUP), 0.1),
        'ffn_up': nrm(ks[20], (DEPTH, D_MODEL, 2 * D_FF), D_MODEL ** -0.5),
        'ffn_conv': nrm(ks[21], (DEPTH, CONV_W, D_FF), CONV_W ** -0.5),
        'ffn_down': nrm(ks[22], (DEPTH, D_FF, D_MODEL), D_FF ** -0.5),
        'final_norm': 1.0 + nrm(ks[23], (D_MODEL,), 0.1),
    }


def reference(x, c, ctx, c_ctx, w_mod, b_mod, norm1, norm2, w_in, w_out, gla_wa2, gla_ba, gla_norm,
              mlstm_conv, mlstm_gate_b, mlstm_norm, sc_conv, cm_ws, cm_bs, cm_norm,
              ffn_up, ffn_conv, ffn_down, final_norm):
    col_sizes = [GLA_COLS, MLSTM_COLS, SC_COLS, CM_COLS]
    for l in range(DEPTH):
        need_ctx = l < DEPTH - 1
        sh1, sc1, g1, sh2, sc2, g2 = modulation(c, w_mod[l], b_mod[l])
        csh1, csc1, cg1, csh2, csc2, cg2 = modulation(c_ctx, w_mod[l], b_mod[l])
        hl = rmsnorm(x, norm1[l]) * (1.0 + sc1) + sh1
        hc = rmsnorm(ctx, norm1[l]) * (1.0 + csc1) + csh1
        pl_gla, pl_ml, pl_sc, pl_cm = split_cols(hl @ w_in[l], col_sizes)
        pc_gla, pc_ml, pc_sc, pc_cm = split_cols(hc @ w_in[l], col_sizes)
        yc_gla, yl_gla = gla_mixer(pc_gla, pl_gla, gla_wa2[l], gla_ba[l], gla_norm[l], need_ctx)
        yc_ml, yl_ml = mlstm_mixer(pc_ml, pl_ml, mlstm_conv[l], mlstm_gate_b[l], mlstm_norm[l], need_ctx)
        yl = jnp.concatenate([yl_gla, yl_ml,
                              short_conv(pl_sc, sc_conv[l], True),
                              chunk_mlp(pl_cm, cm_ws[l], cm_bs[l], cm_norm[l])], axis=-1) @ w_out[l]
        x = x + g1 * yl
        hl2 = rmsnorm(x, norm2[l]) * (1.0 + sc2) + sh2
        x = x + g2 * conv_ffn(hl2, ffn_up[l], ffn_conv[l], ffn_down[l], True)
        if need_ctx:
            yc = jnp.concatenate([yc_gla, yc_ml,
                                  short_conv(pc_sc, sc_conv[l], False),
                                  chunk_mlp(pc_cm, cm_ws[l], cm_bs[l], cm_norm[l])], axis=-1) @ w_out[l]
            ctx = ctx + cg1 * yc
            hc2 = rmsnorm(ctx, norm2[l]) * (1.0 + csc2) + csh2
            ctx = ctx + cg2 * conv_ffn(hc2, ffn_up[l], ffn_conv[l], ffn_down[l], False)
    return rmsnorm(x, final_norm)
```

```python
import contextlib
import math
import numpy as np
import concourse.bass as bass
import concourse.mybir as mybir
from concourse.bass_utils import run_bass_kernel_spmd

F32 = mybir.dt.float32
BF16 = mybir.dt.bfloat16
AF = mybir.ActivationFunctionType
ALU = mybir.AluOpType

D = 2048
KC = 16
T = 2304
NT = 18
HD = 128
DFF = 5632
NFB = 44
OFF_GLA = 0
OFF_ML = 2080
OFF_SC = 4144
OFF_CM = 5680
TG = [(0, 256), (256, 512), (768, 512), (1280, 512), (1792, 512)]
EPS = 1e-6
NDMA_SEM = 6
QS = HD ** -0.5


class Prog:
    def __init__(self, nc, same_engine_sync=True):
        self.nc = nc
        self.ops = []
        self.same_engine_sync = same_engine_sync

    def op(self, eng, fn, reads=(), writes=(), dma=False):
        self.ops.append((eng, fn, tuple(reads), tuple(writes), dma))

    def pe(self, fn, r=(), w=()):
        self.op('pe', fn, r, w)

    def act(self, fn, r=(), w=()):
        self.op('act', fn, r, w)

    def dve(self, fn, r=(), w=()):
        self.op('dve', fn, r, w)

    def pool(self, fn, r=(), w=()):
        self.op('pool', fn, r, w)

    def dma(self, q, fn, r=(), w=()):
        self.op(q, fn, r, w, dma=True)

    def finalize(self, stack):
        nc = self.nc
        ops = self.ops
        n = len(ops)
        dma_ctr = {}
        dom = [None] * n
        pos = [0] * n
        dom_len = {}
        extra_dep = [None] * n
        last_in_dom = {}
        for i, (eng, fn, r, w, dma) in enumerate(ops):
            if dma:
                k = dma_ctr.get(eng, 0)
                dma_ctr[eng] = k + 1
                d = (eng, k % NDMA_SEM)
            else:
                d = eng
            dom[i] = d
            p = dom_len.get(d, 0) + 1
            dom_len[d] = p
            pos[i] = p
            if dma and d in last_in_dom:
                extra_dep[i] = last_in_dom[d]
            last_in_dom[d] = i
        last_w = {}
        readers = {}
        cur_vc = {}
        vc = [None] * n
        waits = [None] * n
        marked = [False] * n
        for i, (eng, fn, r, w, dma) in enumerate(ops):
            deps = set()
            for res in r:
                j = last_w.get(res)
                if j is not None:
                    deps.add(j)
            for res in w:
                j = last_w.get(res)
                if j is not None:
                    deps.add(j)
                rs = readers.get(res)
                if rs:
                    deps.update(rs)
            if extra_dep[i] is not None:
                deps.add(extra_dep[i])
            cv = cur_vc.setdefault(eng, {})
            my_waits = []
            for j in sorted(deps, reverse=True):
                dj = dom[j]
                if dj == dom[i] and not dma:
                    if eng == 'pe' or not self.same_engine_sync:
                        continue
                if cv.get(dj, 0) >= pos[j]:
                    continue
                my_waits.append(j)
                marked[j] = True
                for kk, vv in vc[j].items():
                    if cv.get(kk, 0) < vv:
                        cv[kk] = vv
            waits[i] = my_waits
            v = dict(cv)
            if v.get(dom[i], 0) < pos[i]:
                v[dom[i]] = pos[i]
            vc[i] = v
            for res in r:
                readers.setdefault(res, set()).add(i)
            for res in w:
                last_w[res] = i
                readers[res] = set()
        count = [0] * n
        cnt = {}
        for i in range(n):
            d = dom[i]
            if ops[i][4]:
                marked[i] = True
            if marked[i]:
                c = cnt.get(d, 0) + (16 if ops[i][4] else 1)
                cnt[d] = c
                count[i] = c
        sems = {}
        for d in dom_len:
            nm = d if isinstance(d, str) else f"{d[0]}{d[1]}"
            sems[d] = stack.enter_context(nc.semaphore("s_" + nm))
        self.stats = dict(n_ops=n, n_waits=sum(len(x) for x in waits), n_marked=sum(marked),
                          max_count=max(cnt.values()) if cnt else 0)
        by_eng = {}
        for i in range(n):
            by_eng.setdefault(ops[i][0], []).append(i)
        block = stack.enter_context(nc.Block())
        engmap = {'pe': 'tensor', 'act': 'scalar', 'dve': 'vector', 'pool': 'gpsimd', 'sp': 'sync'}

        def make(e, idxs):
            def body(engine):
                for i in idxs:
                    wd = {}
                    for j in waits[i]:
                        dj = dom[j]
                        if wd.get(dj, 0) < count[j]:
                            wd[dj] = count[j]
                    for dj, cval in wd.items():
                        engine.wait_ge(sems[dj], cval)
                    ins = ops[i][1](engine)
                    if marked[i]:
                        ins.then_inc(sems[dom[i]], 16 if ops[i][4] else 1)
                for d, c in cnt.items():
                    if not isinstance(d, str) and d[0] == e:
                        engine.wait_ge(sems[d], c)
            return body

        for e, idxs in by_eng.items():
            getattr(block, engmap[e])(make(e, idxs))


class V:
    def __init__(self, ap, keys):
        self.ap = ap
        self.keys = keys


class Arena:
    def __init__(self, t, nbytes, name):
        self.t = t
        self.n = nbytes
        self.p = 0
        self.name = name

    def reset(self):
        self.p = 0

    def alloc(self, shape, dt):
        esz = 4 if dt == F32 else 2
        nel = 1
        for s in shape:
            nel *= s
        nb = nel * esz
        off = (self.p + 63) // 64 * 64
        assert off + nb <= self.n, (self.name, off, nb, self.n)
        self.p = off + nb
        ap = self.t[:, off // 2:(off + nb) // 2]
        if dt == F32:
            ap = ap.bitcast(F32)
        if len(shape) > 1:
            names = [f"d{i}" for i in range(len(shape))]
            ap = ap.rearrange("p (" + " ".join(names) + ") -> p " + " ".join(names),
                              **{nm: s for nm, s in zip(names, shape)})
        keys = [(self.name, k) for k in range(off // 512, (off + nb - 1) // 512 + 1)]
        return V(ap, keys)


def build(mode):
    A = mode == 'A'
    nc = bass.Bass("TRN2", target_bir_lowering=False)

    def din(name, shape, dt=F32):
        return nc.dram_tensor(name, shape, dt, kind="ExternalInput").ap()

    def dout(name, shape, dt=F32):
        return nc.dram_tensor(name, shape, dt, kind="ExternalOutput").ap()

    def dscr(name, shape, dt):
        return nc.dram_tensor(name, shape, dt).ap()

    xin = din("xin", [T, D])
    norm1 = din("norm1", [1, D])
    w_in = din("w_in", [D, 6704])
    gla_wa2 = din("gla_wa2", [2, 16, 512])
    gla_ba = din("gla_ba", [2, 512])
    mlstm_conv = din("mlstm_conv", [3, 1024])
    mlstm_gate_b = din("mlstm_gate_b", [1, 16])
    if A:
        c2 = din("c2", [2, D])
        w_mod = din("w_mod", [D, 12288])
        b_mod = din("b_mod", [1, 12288])
        modrow = dout("modrow", [2, 12288])
        st_ctx = dout("st_ctx", [2, 2, 128, 516])
        st_loc = dout("st_loc", [2, 2, 128, 516])
        st_bt = dout("st_bt", [2, 2, 128, 4])
    else:
        modrow = din("modrow", [2, 12288])
        norm2 = din("norm2", [1, D])
        w_out = din("w_out", [D, D])
        gla_norm = din("gla_norm", [1, 512])
        mlstm_norm = din("mlstm_norm", [1, 512])
        sc_conv = din("sc_conv", [3, 512])
        cm_ws = din("cm_ws", [4, 128, 128])
        cm_bs = din("cm_bs", [4, 128])
        cm_norm = din("cm_norm", [1, 512])
        ffn_up = din("ffn_up", [D, 2 * DFF])
        ffn_conv = din("ffn_conv", [3, DFF])
        ffn_down = din("ffn_down", [DFF, D])
        final_norm = din("final_norm", [1, D])
        in_s0 = din("in_s0", [2, 2, 128, 516])
        in_loc = din("in_loc", [2, 2, 3, 128, 516])
        in_bt = din("in_bt", [2, 2, 3, 128, 4])
        xout = dout("xout", [T, D])
        yfinal = dout("yfinal", [2048, D])
        import os as _os
        _dbg = bool(_os.environ.get("MK_DEBUG"))
        x1 = dout("x1", [T, D]) if _dbg else dscr("x1", [T, D], F32)
        yT = dout("yT", [16, 128, T], BF16) if _dbg else dscr("yT", [16, 128, T], BF16)
        actT = dscr("actT", [NFB, 128, T], BF16)
    scr = {}
    for m in ("g", "m"):
        scr["kT" + m] = dscr("kT" + m, [128, 4, T], BF16)
        scr["v" + m] = dscr("v" + m, [T, 512], BF16)
        if not A:
            scr["qT" + m] = dscr("qT" + m, [128, 4, T], BF16)
            scr["gate" + m] = dscr("gate" + m, [T, 512], BF16)
            scr["oacc" + m] = dscr("oacc" + m, [T, 516], F32)
    zT = dscr("zT", [32, T], F32)
    gatesd = dscr("gatesd", [T, 16], F32)

    st = contextlib.ExitStack()
    with st:
        def sb(name, shape, dt):
            return st.enter_context(nc.sbuf_tensor(name, shape, dt))

        P = Prog(nc)
        big = sb("big", [128, KC, T], BF16)
        wbuf = sb("wbuf", [128, 2, KC, 512], BF16)
        ar_t = sb("ar", [128, 22528], BF16)
        ar = Arena(ar_t, 45056, "ar")
        cn_t = sb("cn", [128, 12288], BF16)
        cn = Arena(cn_t, 24576, "cn")
        ps = st.enter_context(nc.psum_tensor("ps", [128, 8, 512], F32))

        def bank(i):
            return V(ps[:, i, :], [("ps", i)])

        def bank_bf(i):
            return V(ps[:, i, :].bitcast(BF16), [("ps", i)])

        hkeys = [("hT", tt) for tt in range(NT)]

        identf = cn.alloc([128], F32)
        ident = cn.alloc([128], BF16)
        tri = [cn.alloc([128], F32), cn.alloc([128], F32)]
        P.pool(lambda e: e.memset(identf.ap, 1.0), w=identf.keys)
        P.pool(lambda e: e.affine_select(out=identf.ap, in_=identf.ap, pattern=[[-1, 128]], compare_op=ALU.is_equal,
                                         fill=0.0, base=0, channel_multiplier=1), r=identf.keys, w=identf.keys)
        P.dve(lambda e: e.tensor_copy(out=ident.ap, in_=identf.ap), r=identf.keys, w=ident.keys)
        for k, sgn in enumerate((-1, 1)):
            P.pool(lambda e, k=k: e.memset(tri[k].ap, 1.0), w=tri[k].keys)
            P.pool(lambda e, k=k, sgn=sgn: e.affine_select(out=tri[k].ap, in_=tri[k].ap, pattern=[[-sgn, 128]], compare_op=ALU.is_ge,
                                                            fill=0.0, base=0, channel_multiplier=sgn), r=tri[k].keys, w=tri[k].keys)
        U = {}
        for m, cval in (("g", -1.0 / 16.0), ("m", -1.0)):
            for dirn in range(2):
                uq = cn.alloc([128], F32)
                uk = cn.alloc([128], F32)
                P.dve(lambda e, uq=uq, dirn=dirn, cval=cval: e.tensor_scalar(out=uq.ap, in0=tri[dirn].ap, scalar1=cval, scalar2=None, op0=ALU.mult),
                      r=tri[dirn].keys, w=uq.keys)
                P.dve(lambda e, uk=uk, dirn=dirn, cval=cval: e.tensor_scalar(out=uk.ap, in0=tri[dirn].ap, scalar1=-cval, scalar2=None, op0=ALU.mult),
                      r=tri[dirn].keys, w=uk.keys)
                U[(m, dirn)] = (uq, uk)
        masks = []
        if not A:
            for dirn in range(2):
                mk = cn.alloc([4, 128], BF16)
                for h in range(4):
                    P.dve(lambda e, mk=mk, h=h, dirn=dirn: e.tensor_copy(out=mk.ap[:, h, :], in_=tri[dirn].ap), r=tri[dirn].keys, w=mk.keys)
                masks.append(mk)
        Wa = []
        for dirn in range(2):
            wa = cn.alloc([512], F32)
            P.dve(lambda e, wa=wa: e.memset(wa.ap[0:32, :], 0.0), w=wa.keys)
            P.dma('sp', lambda e, wa=wa, dirn=dirn: e.dma_start(out=wa.ap[16 * dirn:16 * dirn + 16, :], in_=gla_wa2[dirn]), r=wa.keys, w=wa.keys)
            P.dma('sp', lambda e, wa=wa, dirn=dirn: e.dma_start(out=wa.ap[32:33, :], in_=gla_ba[dirn:dirn + 1, :]), r=wa.keys, w=wa.keys)
            Wa.append(wa)
        gb = cn.alloc([16], F32)
        P.dma('sp', lambda e: e.dma_start(out=gb.ap, in_=mlstm_gate_b.partition_broadcast(128)), w=gb.keys)
        mcw = cn.alloc([8, 3], F32)
        for j in range(3):
            P.dma('sp', lambda e, j=j: e.dma_start(out=mcw.ap[:, :, j], in_=mlstm_conv[j:j + 1, :].rearrange("r (b p) -> p (r b)", p=128), allow_slow_non_contiguous=True), w=mcw.keys)
        vec = {}

        def fmvec(name, src_row_ap):
            v = cn.alloc([KC], F32)
            P.dma('sp', lambda e: e.dma_start(out=v.ap, in_=src_row_ap.rearrange("r (kc p) -> p (r kc)", p=128), allow_slow_non_contiguous=True),
                  r=["modrow"], w=v.keys)
            vec[name] = v
            return v

        if A:
            ar.reset()
            cT = ar.alloc([KC, 2], F32)
            cTb = ar.alloc([KC, 2], BF16)
            for r in range(2):
                P.dma('sp', lambda e, r=r: e.dma_start(out=cT.ap[:, :, r], in_=c2[r:r + 1, :].rearrange("r (kc p) -> p (r kc)", p=128), allow_slow_non_contiguous=True), w=cT.keys)
            P.act(lambda e: e.activation(out=cTb.ap, in_=cT.ap, func=AF.Silu), r=cT.keys, w=cTb.keys)
            mst = [ar.alloc([512], F32), ar.alloc([512], F32)]
            bmt = [ar.alloc([512], F32), ar.alloc([512], F32)]
            for ci in range(24):
                wt = V(wbuf[:, ci % 2], [("wbuf", ci % 2)])
                for half in range(2):
                    P.dma('pool', lambda e, wt=wt, ci=ci, half=half: e.dma_start(
                        out=wt.ap[:, 8 * half:8 * half + 8, :],
                        in_=w_mod[:, ci * 512:(ci + 1) * 512].rearrange("(kc p) n -> p kc n", p=128)[:, 8 * half:8 * half + 8, :]),
                        w=wt.keys)
                bm = bmt[ci % 2]
                P.dma('sp', lambda e, bm=bm, ci=ci: e.dma_start(out=bm.ap[0:2, :], in_=b_mod[:, ci * 512:(ci + 1) * 512].partition_broadcast(2)), w=bm.keys)
                pb = bank(ci % 2)
                for kc in range(KC):
                    P.pe(lambda e, pb=pb, wt=wt, kc=kc: e.matmul(pb.ap[0:2, :], lhsT=cTb.ap[:, kc, :], rhs=wt.ap[:, kc, :], start=(kc == 0), stop=(kc == KC - 1)),
                         r=cTb.keys + wt.keys, w=pb.keys)
                ms = mst[ci % 2]
                P.dve(lambda e, ms=ms, pb=pb, bm=bm: e.tensor_tensor(out=ms.ap[0:2, :], in0=pb.ap[0:2, :], in1=bm.ap[0:2, :], op=ALU.add),
                      r=pb.keys + bm.keys, w=ms.keys)
                P.dma('sp', lambda e, ms=ms, ci=ci: e.dma_start(out=modrow[:, ci * 512:(ci + 1) * 512], in_=ms.ap[0:2, :]), r=ms.keys, w=["modrow"])

        n1T = cn.alloc([KC], F32)
        P.dma('sp', lambda e: e.dma_start(out=n1T.ap, in_=norm1.rearrange("r (kc p) -> p (r kc)", p=128), allow_slow_non_contiguous=True), w=n1T.keys)
        if not A:
            n2T = cn.alloc([KC], F32)
            P.dma('sp', lambda e: e.dma_start(out=n2T.ap, in_=norm2.rearrange("r (kc p) -> p (r kc)", p=128), allow_slow_non_contiguous=True), w=n2T.keys)

        def modvecs(which):
            nT = n1T if which == 1 else n2T
            off_sh = 0 if which == 1 else 6144
            off_sc = 2048 if which == 1 else 8192
            out = []
            for r in range(2):
                sc = fmvec(f"sc{which}{r}", modrow[r:r + 1, off_sc:off_sc + D])
                sh = fmvec(f"sh{which}{r}", modrow[r:r + 1, off_sh:off_sh + D])
                P.dve(lambda e, sc=sc, nT=nT: e.scalar_tensor_tensor(out=sc.ap, in0=sc.ap, scalar=1.0, in1=nT.ap, op0=ALU.add, op1=ALU.mult),
                      r=sc.keys + nT.keys, w=sc.keys)
                out.append((sc, sh))
            return out

        def build_hT(src, mv):
            ar.reset()
            xts = [ar.alloc([D], F32), ar.alloc([D], F32)]
            xns = [ar.alloc([D], BF16), ar.alloc([D], BF16)]
            st4 = [ar.alloc([4], F32), ar.alloc([4], F32)]
            for tt in range(NT):
                xt = xts[tt % 2]
                xn = xns[tt % 2]
                s4 = st4[tt % 2]
                scale, shift = mv[1] if tt < 2 else mv[0]
                P.dma('sp', lambda e, xt=xt, tt=tt: e.dma_start(out=xt.ap, in_=src[tt * 128:(tt + 1) * 128, :]), r=[("xsrc", tt)], w=xt.keys)
                P.dve(lambda e, s4=s4: e.memset(s4.ap, 0.0), w=s4.keys)
                P.act(lambda e, xt=xt, xn=xn, s4=s4: e.activation(out=xn.ap, in_=xt.ap, func=AF.Square, accum_out=s4.ap[:, 0:1]),
                      r=xt.keys + s4.keys, w=xn.keys + s4.keys)
                P.act(lambda e, s4=s4: e.activation(out=s4.ap[:, 1:2], in_=s4.ap[:, 0:1], func=AF.Sqrt, scale=1.0 / D, bias=EPS), r=s4.keys, w=s4.keys)
                P.dve(lambda e, s4=s4: e.reciprocal(out=s4.ap[:, 2:3], in_=s4.ap[:, 1:2]), r=s4.keys, w=s4.keys)
                P.dve(lambda e, xt=xt, xn=xn, s4=s4: e.tensor_scalar(out=xn.ap, in0=xt.ap, scalar1=s4.ap[:, 2:3], scalar2=None, op0=ALU.mult),
                      r=xt.keys + s4.keys, w=xn.keys)
                for half in range(2):
                    pb = bank_bf(2 * (tt % 2) + half)
                    for k8 in range(8):
                        kc = half * 8 + k8
                        P.pe(lambda e, pb=pb, xn=xn, kc=kc, k8=k8: e.transpose(pb.ap[:, k8 * 128:(k8 + 1) * 128], xn.ap[:, kc * 128:(kc + 1) * 128], ident.ap),
                             r=xn.keys + ident.keys, w=pb.keys)
                    for k8 in range(8):
                        kc = half * 8 + k8
                        if k8 % 2 == 0:
                            P.act(lambda e, pb=pb, kc=kc, k8=k8, tt=tt, scale=scale, shift=shift: e.activation(
                                out=big[:, kc, tt * 128:(tt + 1) * 128], in_=pb.ap[:, k8 * 128:(k8 + 1) * 128], func=AF.Identity,
                                scale=scale.ap[:, kc:kc + 1], bias=shift.ap[:, kc:kc + 1]),
                                r=pb.keys + scale.keys + shift.keys, w=[("hT", tt)])
                        else:
                            P.dve(lambda e, pb=pb, kc=kc, k8=k8, tt=tt, scale=scale, shift=shift: e.tensor_scalar(
                                out=big[:, kc, tt * 128:(tt + 1) * 128], in0=pb.ap[:, k8 * 128:(k8 + 1) * 128],
                                scalar1=scale.ap[:, kc:kc + 1], scalar2=shift.ap[:, kc:kc + 1], op0=ALU.mult, op1=ALU.add),
                                r=pb.keys + scale.keys + shift.keys, w=[("hT", tt)])

        wctr = [0]

        def loadw(srcs, nkc=KC):
            i = wctr[0] % 2
            wctr[0] += 1
            wt = V(wbuf[:, i], [("wbuf", i)])
            for (src, off, n) in srcs:
                for half in range(2):
                    P.dma('pool', lambda e, wt=wt, src=src, off=off, n=n, half=half: e.dma_start(
                        out=wt.ap[:, 8 * half:8 * half + 8, off:off + n],
                        in_=src.rearrange("(kc p) n -> p kc n", p=128)[:, 8 * half:8 * half + 8, :]), w=wt.keys)
            return wt

        def fm(wt, coff, m, tg, pb):
            lo, n = tg
            tts = list(range(lo // 128, (lo + n) // 128))
            for kc in range(KC):
                P.pe(lambda e, kc=kc: e.matmul(pb.ap[0:m, 0:n], lhsT=wt.ap[:, kc, coff:coff + m], rhs=big[:, kc, lo:lo + n],
                                               start=(kc == 0), stop=(kc == KC - 1)),
                     r=wt.keys + [("hT", t) for t in tts], w=pb.keys)

        def tm(wt, coff, ncol, tt, pb):
            for kc in range(KC):
                P.pe(lambda e, kc=kc: e.matmul(pb.ap[:, 0:ncol], lhsT=big[:, kc, tt * 128:(tt + 1) * 128], rhs=wt.ap[:, kc, coff:coff + ncol],
                                               start=(kc == 0), stop=(kc == KC - 1)),
                     r=wt.keys + [("hT", tt)], w=pb.keys)

        pctr = [0]

        def nbank(lo=0, hi=8):
            i = lo + pctr[0] % (hi - lo)
            pctr[0] += 1
            return bank(i)

        def conv3(src, dst, n, cw, lo):
            W = 256 if lo == 0 else 64
            R = n // W
            s3 = src[:, 0:n].rearrange("p (r w) -> p r w", w=W)
            d3 = dst[:, 0:n].rearrange("p (r w) -> p r w", w=W)
            return [
                lambda e: e.tensor_scalar(out=dst[:, 0:n], in0=src[:, 0:n], scalar1=cw[:, 1:2], scalar2=None, op0=ALU.mult),
                lambda e: e.scalar_tensor_tensor(out=d3[:, :, 1:W], in0=s3[:, :, 0:W - 1], scalar=cw[:, 0:1], in1=d3[:, :, 1:W], op0=ALU.mult, op1=ALU.add),
                lambda e: e.scalar_tensor_tensor(out=d3[:, :, 0:W - 1], in0=s3[:, :, 1:W], scalar=cw[:, 2:3], in1=d3[:, :, 0:W - 1], op0=ALU.mult, op1=ALU.add),
            ]

        def inproj():
            ar.reset()
            stg = [ar.alloc([512], BF16) for _ in range(3)]
            stf = [ar.alloc([512], F32) for _ in range(2)]
            cvf = [ar.alloc([512], F32) for _ in range(2)]
            sctr = [0]

            def nstg():
                s = stg[sctr[0] % 3]
                sctr[0] += 1
                return s

            def fm_to_scratch(wt, nblk, dst, scale=None, conv=None):
                for hb in range(nblk):
                    for gi, tg in enumerate(TG):
                        lo, n = tg
                        pb = nbank(0, 4)
                        fm(wt, hb * 128, 128, tg, pb)
                        s = nstg()
                        if conv is None:
                            if scale is None:
                                P.act(lambda e, s=s, pb=pb, n=n: e.activation(func=AF.Identity, out=s.ap[:, 0:n], in_=pb.ap[:, 0:n]), r=pb.keys, w=s.keys)
                            else:
                                P.act(lambda e, s=s, pb=pb, n=n: e.mul(out=s.ap[:, 0:n], in_=pb.ap[:, 0:n], mul=scale), r=pb.keys, w=s.keys)
                        else:
                            a = stf[sctr[0] % 2]
                            c = cvf[sctr[0] % 2]
                            P.act(lambda e, a=a, pb=pb, n=n: e.mul(out=a.ap[:, 0:n], in_=pb.ap[:, 0:n], mul=(scale or 1.0)), r=pb.keys, w=a.keys)
                            fns = conv3(a.ap, c.ap, n, conv[:, hb, :], lo)
                            for fn in fns:
                                P.dve(fn, r=a.keys + c.keys + mcw.keys, w=c.keys)
                            P.act(lambda e, s=s, c=c, n=n: e.activation(func=AF.Identity, out=s.ap[:, 0:n], in_=c.ap[:, 0:n]), r=c.keys, w=s.keys)
                        P.dma('sp', lambda e, s=s, hb=hb, lo=lo, n=n: e.dma_start(out=dst[:, hb, lo:lo + n], in_=s.ap[:, 0:n]), r=s.keys, w=[(id(dst), lo)])

            def tm_to_scratch(wt, dst, func=None):
                for tt in range(NT):
                    pb = nbank(0, 4)
                    tm(wt, 0, 512, tt, pb)
                    s = nstg()
                    if func is None:
                        if tt % 2 == 0:
                            P.act(lambda e, s=s, pb=pb: e.activation(func=AF.Identity, out=s.ap, in_=pb.ap), r=pb.keys, w=s.keys)
                        else:
                            P.dve(lambda e, s=s, pb=pb: e.tensor_copy(out=s.ap, in_=pb.ap), r=pb.keys, w=s.keys)
                    else:
                        P.act(lambda e, s=s, pb=pb: e.activation(out=s.ap, in_=pb.ap, func=func), r=pb.keys, w=s.keys)
                    P.dma('sp', lambda e, s=s, tt=tt: e.dma_start(out=dst[tt * 128:(tt + 1) * 128, :], in_=s.ap), r=s.keys, w=[(id(dst), tt * 128)])

            def wcols(c0, n):
                return [(w_in[:, c0:c0 + n], 0, n)]

            if not A:
                fm_to_scratch(loadw(wcols(OFF_GLA, 512)), 4, scr["qTg"], scale=QS)
            fm_to_scratch(loadw(wcols(OFF_GLA + 512, 512)), 4, scr["kTg"])
            tm_to_scratch(loadw(wcols(OFF_GLA + 1024, 512)), scr["vg"])
            if not A:
                tm_to_scratch(loadw(wcols(OFF_GLA + 1536, 512)), scr["gateg"], func=AF.Silu)
            wt = loadw(wcols(OFF_GLA + 2048, 32))
            for tg in TG:
                lo, n = tg
                pb = nbank(0, 4)
                fm(wt, 0, 32, tg, pb)
                a = stf[sctr[0] % 2]
                sctr[0] += 1
                P.act(lambda e, a=a, pb=pb, n=n: e.activation(func=AF.Identity, out=a.ap[0:32, 0:n], in_=pb.ap[0:32, 0:n]), r=pb.keys, w=a.keys)
                P.dma('sp', lambda e, a=a, lo=lo, n=n: e.dma_start(out=zT[:, lo:lo + n], in_=a.ap[0:32, 0:n]), r=a.keys, w=[("zT", lo)])
            if not A:
                fm_to_scratch(loadw(wcols(OFF_ML, 512)), 4, scr["qTm"], conv=mcw.ap[:, 0:4, :])
            fm_to_scratch(loadw(wcols(OFF_ML + 512, 512)), 4, scr["kTm"], scale=QS, conv=mcw.ap[:, 4:8, :])
            tm_to_scratch(loadw(wcols(OFF_ML + 1024, 512)), scr["vm"])
            if not A:
                tm_to_scratch(loadw(wcols(OFF_ML + 1536, 512)), scr["gatem"], func=AF.Sigmoid)
            wt = loadw(wcols(OFF_ML + 2048, 16))
            for tt in range(NT):
                pb = nbank(0, 4)
                tm(wt, 0, 16, tt, pb)
                a = stf[sctr[0] % 2]
                sctr[0] += 1
                P.act(lambda e, a=a, pb=pb: e.activation(func=AF.Identity, out=a.ap[:, 0:16], in_=pb.ap[:, 0:16]), r=pb.keys, w=a.keys)
                P.dma('sp', lambda e, a=a, tt=tt: e.dma_start(out=gatesd[tt * 128:(tt + 1) * 128, :], in_=a.ap[:, 0:16]), r=a.keys, w=[("gatesd", tt)])
            if A:
                return
            scw = ar.alloc([4, 3], F32)
            for j in range(3):
                P.dma('sp', lambda e, j=j: e.dma_start(out=scw.ap[:, :, j], in_=sc_conv[j:j + 1, :].rearrange("r (b p) -> p (r b)", p=128), allow_slow_non_contiguous=True), w=scw.keys)
            for cb in range(4):
                wt = loadw([(w_in[:, OFF_SC + k * 512 + cb * 128: OFF_SC + k * 512 + cb * 128 + 128], k * 128, 128) for k in range(3)])
                for tg in TG:
                    lo, n = tg
                    pbs = [nbank(0, 4) for _ in range(3)]
                    for k in range(3):
                        fm(wt, k * 128, 128, tg, pbs[k])
                    a = stf[sctr[0] % 2]
                    c = cvf[sctr[0] % 2]
                    s = nstg()
                    P.act(lambda e, a=a, n=n, pb=pbs[1]: e.activation(func=AF.Identity, out=a.ap[:, 0:n], in_=pb.ap[:, 0:n]), r=pbs[1].keys, w=a.keys)
                    P.dve(lambda e, a=a, n=n, pb=pbs[2]: e.tensor_tensor(out=a.ap[:, 0:n], in0=a.ap[:, 0:n], in1=pb.ap[:, 0:n], op=ALU.mult),
                          r=a.keys + pbs[2].keys, w=a.keys)
                    for fn in conv3(a.ap, c.ap, n, scw.ap[:, cb, :], lo):
                        P.dve(fn, r=a.keys + c.keys + scw.keys, w=c.keys)
                    P.dve(lambda e, s=s, c=c, n=n, pb=pbs[0]: e.tensor_tensor(out=s.ap[:, 0:n], in0=c.ap[:, 0:n], in1=pb.ap[:, 0:n], op=ALU.mult),
                          r=c.keys + pbs[0].keys, w=s.keys)
                    P.dma('sp', lambda e, s=s, cb=cb, lo=lo, n=n: e.dma_start(out=yT[8 + cb, :, lo:lo + n], in_=s.ap[:, 0:n]), r=s.keys, w=[("yT", 8 + cb, lo)])
            wsf = ar.alloc([4, 128], F32)
            wsT = ar.alloc([4, 128], BF16)
            bsT = ar.alloc([4], F32)
            cmn = ar.alloc([512], F32)
            P.dma('sp', lambda e: e.dma_start(out=wsf.ap, in_=cm_ws.rearrange("g t s -> t g s")), w=wsf.keys)
            P.dma('sp', lambda e: e.dma_start(out=bsT.ap, in_=cm_bs.rearrange("g t -> t g"), allow_slow_non_contiguous=True), w=bsT.keys)
            P.dma('sp', lambda e: e.dma_start(out=cmn.ap, in_=cm_norm.partition_broadcast(128)), w=cmn.keys)
            pbw = bank(4)
            for g in range(4):
                P.pe(lambda e, g=g: e.matmul(pbw.ap[:, g * 128:(g + 1) * 128], lhsT=wsf.ap[:, g, :], rhs=identf.ap, start=True, stop=True),
                     r=wsf.keys + identf.keys, w=pbw.keys)
            P.dve(lambda e: e.tensor_copy(out=wsT.ap.rearrange("p g t -> p (g t)"), in_=pbw.ap), r=pbw.keys, w=wsT.keys)
            wv = loadw(wcols(OFF_CM + 512, 512))
            wu = loadw(wcols(OFF_CM, 512))
            gv = ar.alloc([512], F32)
            gu = ar.alloc([512], F32)
            t5 = ar.alloc([512], F32)
            vn = ar.alloc([512], BF16)
            ycm = ar.alloc([512], BF16)
            yt4 = ar.alloc([4, 128], BF16)
            s4 = ar.alloc([4], F32)
            GC = 2.0 * math.sqrt(2.0 / math.pi)

            def gelu(dst, pb):
                P.act(lambda e: e.activation(out=t5.ap, in_=pb.ap, func=AF.Square), r=pb.keys, w=t5.keys)
                P.dve(lambda e: e.tensor_scalar(out=t5.ap, in0=t5.ap, scalar1=0.044715, scalar2=1.0, op0=ALU.mult, op1=ALU.add), r=t5.keys, w=t5.keys)
                P.dve(lambda e: e.tensor_tensor(out=t5.ap, in0=t5.ap, in1=pb.ap, op=ALU.mult), r=t5.keys + pb.keys, w=t5.keys)
                P.act(lambda e: e.activation(out=t5.ap, in_=t5.ap, func=AF.Sigmoid, scale=GC), r=t5.keys, w=t5.keys)
                P.dve(lambda e: e.tensor_tensor(out=dst.ap, in0=t5.ap, in1=pb.ap, op=ALU.mult), r=t5.keys + pb.keys, w=dst.keys)

            for tt in range(NT):
                pv = bank(0 + 2 * (tt % 2))
                pu = bank(1 + 2 * (tt % 2))
                tm(wv, 0, 512, tt, pv)
                tm(wu, 0, 512, tt, pu)
                gelu(gv, pv)
                gelu(gu, pu)
                P.dve(lambda e: e.memset(s4.ap, 0.0), w=s4.keys)
                P.act(lambda e: e.activation(out=t5.ap, in_=gv.ap, func=AF.Square, accum_out=s4.ap[:, 0:1]), r=gv.keys + s4.keys, w=t5.keys + s4.keys)
                P.act(lambda e: e.activation(out=s4.ap[:, 1:2], in_=s4.ap[:, 0:1], func=AF.Sqrt, scale=1.0 / 512, bias=EPS), r=s4.keys, w=s4.keys)
                P.dve(lambda e: e.reciprocal(out=s4.ap[:, 2:3], in_=s4.ap[:, 1:2]), r=s4.keys, w=s4.keys)
                P.dve(lambda e: e.scalar_tensor_tensor(out=vn.ap, in0=gv.ap, scalar=s4.ap[:, 2:3], in1=cmn.ap, op0=ALU.mult, op1=ALU.mult),
                      r=gv.keys + s4.keys + cmn.keys, w=vn.keys)
                pS = bank(4 + tt % 2)
                for g in range(4):
                    P.pe(lambda e, g=g, pS=pS: e.matmul(pS.ap[:, g * 128:(g + 1) * 128], lhsT=wsT.ap[:, g, :], rhs=vn.ap[:, g * 128:(g + 1) * 128], start=True, stop=True),
                         r=wsT.keys + vn.keys, w=pS.keys)
                for g in range(4):
                    P.dve(lambda e, g=g, pS=pS: e.scalar_tensor_tensor(out=ycm.ap[:, g * 128:(g + 1) * 128], in0=pS.ap[:, g * 128:(g + 1) * 128],
                                                                         scalar=bsT.ap[:, g:g + 1], in1=gu.ap[:, g * 128:(g + 1) * 128], op0=ALU.add, op1=ALU.mult),
                          r=pS.keys + bsT.keys + gu.keys, w=ycm.keys)
                pT = bank_bf(6 + tt % 2)
                for g in range(4):
                    P.pe(lambda e, g=g, pT=pT: e.transpose(pT.ap[:, g * 128:(g + 1) * 128], ycm.ap[:, g * 128:(g + 1) * 128], ident.ap),
                         r=ycm.keys + ident.keys, w=pT.keys)
                P.act(lambda e, pT=pT: e.activation(func=AF.Identity, out=yt4.ap.rearrange("p g t -> p (g t)"), in_=pT.ap[:, 0:512]), r=pT.keys, w=yt4.keys)
                P.dma('sp', lambda e, tt=tt: e.dma_start(out=yT[12:16, :, tt * 128:(tt + 1) * 128].rearrange("k p t -> p k t"), in_=yt4.ap),
                      r=yt4.keys, w=[("yT", 12, tt)])

        def scans():
            for mi, m in enumerate(("g", "m")):
                scan_mixer(mi, m)

        def scan_mixer(mi, m):
            if True:
                ml = m == "m"
                ar.reset()
                S = ar.alloc([4, 129], F32)
                Sbf = ar.alloc([4, 129], BF16)
                bt = ar.alloc([4], F32)
                kcs = [ar.alloc([4, 128], BF16) for _ in range(2)]
                vcs = [ar.alloc([4, 129], BF16) for _ in range(2)]
                for v in vcs:
                    P.dve(lambda e, v=v: e.memset(v.ap[:, :, 128:129], 1.0), w=v.keys)
                if ml:
                    gcs = [ar.alloc([16], F32) for _ in range(2)]
                    gs = ar.alloc([16], F32)
                    t4 = ar.alloc([8], F32)
                    Ibc = ar.alloc([4, 128], F32)
                else:
                    zcs = [ar.alloc([128], F32) for _ in range(2)]
                    for z in zcs:
                        P.dve(lambda e, z=z: e.memset(z.ap[32:33, :], 1.0), w=z.keys)
                    t1 = ar.alloc([512], F32)
                Lp = ar.alloc([4, 128], F32)
                ek = ar.alloc([4, 128], F32)
                el = ar.alloc([4], F32)
                knT = ar.alloc([4, 128], BF16)
                kn = ar.alloc([4, 128], BF16)
                if not A:
                    qcs = [ar.alloc([4, 128], BF16) for _ in range(2)]
                    gts = [ar.alloc([512], BF16) for _ in range(2)]
                    oas = [ar.alloc([4, 129], F32) for _ in range(2)]
                    eq = ar.alloc([4, 128], F32)
                    qi = ar.alloc([4, 128], BF16)
                    am = ar.alloc([4, 128], BF16)
                    ot = ar.alloc([4, 129], F32)
                    hv = ar.alloc([4, 128], F32)
                    s8 = ar.alloc([16], F32)
                    yb = ar.alloc([512], BF16)
                    yt4 = ar.alloc([4, 128], BF16)
                    gnb = ar.alloc([4, 128], F32)
                    P.dma('sp', lambda e, gnb=gnb, ml=ml: e.dma_start(out=gnb.ap.rearrange("p h v -> p (h v)"),
                                                                      in_=(mlstm_norm if ml else gla_norm).partition_broadcast(128)), w=gnb.keys)
                    elu = None
                kTd, vd = scr["kT" + m], scr["v" + m]
                uctr = [0]

                def unit(c, dirn, full, second):
                    i2 = uctr[0] % 2
                    uctr[0] += 1
                    tok = c * 128
                    last = 127 if dirn == 0 else 0
                    Uq, Uk = U[(m, dirn)]
                    kc_t, vc_t = kcs[i2], vcs[i2]
                    P.dma('sp', lambda e: e.dma_start(out=kc_t.ap, in_=kTd[:, :, tok:tok + 128]), r=[(id(kTd), TGLO(tok))], w=kc_t.keys)
                    P.dma('sp', lambda e: e.dma_start(out=vc_t.ap[:, :, 0:128], in_=vd[tok:tok + 128, :].rearrange("t (h d) -> t h d", h=4)),
                          r=[(id(vd), tok)], w=vc_t.keys)
                    if full:
                        qc_t = qcs[i2]
                        qTd = scr["qT" + m]
                        P.dma('sp', lambda e: e.dma_start(out=qc_t.ap, in_=qTd[:, :, tok:tok + 128]), r=[(id(qTd), TGLO(tok))], w=qc_t.keys)
                        if second:
                            gt_t, oa_t = gts[i2], oas[i2]
                            gd, od = scr["gate" + m], scr["oacc" + m]
                            P.dma('sp', lambda e: e.dma_start(out=gt_t.ap, in_=gd[tok:tok + 128, :]), r=[(id(gd), tok)], w=gt_t.keys)
                            P.dma('sp', lambda e: e.dma_start(out=oa_t.ap.rearrange("p h v -> p (h v)"), in_=od[tok:tok + 128, :]), r=[(id(od), tok)], w=oa_t.keys)
                    pX, pY = bank(0), bank(1)
                    if not ml:
                        zc_t = zcs[i2]
                        P.dma('sp', lambda e: e.dma_start(out=zc_t.ap[0:32, :], in_=zT[:, tok:tok + 128]), r=[("zT", TGLO(tok))], w=zc_t.keys)
                        pL = bank(2)
                        P.pe(lambda e: e.matmul(pL.ap, lhsT=zc_t.ap[0:33, :], rhs=Wa[dirn].ap[0:33, :], start=True, stop=True),
                             r=zc_t.keys + Wa[dirn].keys, w=pL.keys)
                        P.act(lambda e: e.activation(out=t1.ap, in_=pL.ap, func=AF.Exp, scale=-1.0), r=pL.keys, w=t1.keys)
                        P.act(lambda e: e.activation(out=Lp.ap.rearrange("p h d -> p (h d)"), in_=t1.ap, func=AF.Ln, bias=1.0), r=t1.keys, w=Lp.keys)
                    else:
                        gc_t = gcs[i2]
                        P.dma('sp', lambda e: e.dma_start(out=gc_t.ap, in_=gatesd[tok:tok + 128, :]), r=[("gatesd", c)], w=gc_t.keys)
                        P.dve(lambda e: e.tensor_tensor(out=gs.ap, in0=gc_t.ap, in1=gb.ap, op=ALU.add), r=gc_t.keys + gb.keys, w=gs.keys)
                        P.act(lambda e: e.activation(out=t4.ap[:, 0:4], in_=gs.ap[:, dirn * 8 + 4:dirn * 8 + 8], func=AF.Exp, scale=-1.0), r=gs.keys, w=t4.keys)
                        P.act(lambda e: e.activation(out=t4.ap[:, 4:8], in_=t4.ap[:, 0:4], func=AF.Ln, bias=1.0), r=t4.keys, w=t4.keys)
                        P.dve(lambda e: e.tensor_copy(out=Lp.ap, in_=t4.ap[:, 4:8].unsqueeze(2).to_broadcast([128, 4, 128])), r=t4.keys, w=Lp.keys)
                        P.dve(lambda e: e.tensor_copy(out=Ibc.ap, in_=gs.ap[:, dirn * 8:dirn * 8 + 4].unsqueeze(2).to_broadcast([128, 4, 128])), r=gs.keys, w=Ibc.keys)
                    for h in range(4):
                        P.pe(lambda e, h=h: e.matmul(pX.ap[:, h * 128:(h + 1) * 128], lhsT=Lp.ap[:, h, :], rhs=Uq.ap, start=True, stop=True),
                             r=Lp.keys + Uq.keys, w=pX.keys)
                    for h in range(4):
                        P.pe(lambda e, h=h: e.matmul(pY.ap[:, h * 128:(h + 1) * 128], lhsT=Lp.ap[:, h, :], rhs=Uk.ap, start=True, stop=not ml),
                             r=Lp.keys + Uk.keys, w=pY.keys)
                        if ml:
                            P.pe(lambda e, h=h: e.matmul(pY.ap[:, h * 128:(h + 1) * 128], lhsT=Ibc.ap[:, h, :], rhs=identf.ap, start=False, stop=True),
                                 r=Ibc.keys + identf.keys, w=pY.keys)
                    pX3 = pX.ap.rearrange("p (h t) -> p h t", h=4)
                    if full:
                        P.act(lambda e: e.activation(out=eq.ap.rearrange("p h t -> p (h t)"), in_=pX.ap, func=AF.Exp), r=pX.keys, w=eq.keys)
                    P.act(lambda e: e.activation(out=el.ap, in_=pX3[:, :, last], func=AF.Exp), r=pX.keys, w=el.keys)
                    P.act(lambda e: e.activation(out=ek.ap.rearrange("p h t -> p (h t)"), in_=pY.ap, func=AF.Exp), r=pY.keys, w=ek.keys)
                    if A and c >= 2:
                        P.dve(lambda e: e.tensor_tensor(out=bt.ap, in0=bt.ap, in1=pX3[:, :, last], op=ALU.add), r=bt.keys + pX.keys, w=bt.keys)
                    P.dve(lambda e: e.tensor_tensor(out=knT.ap, in0=kc_t.ap, in1=ek.ap, op=ALU.mult), r=kc_t.keys + ek.keys, w=knT.keys)
                    if full:
                        P.dve(lambda e: e.tensor_tensor(out=qi.ap, in0=qc_t.ap, in1=eq.ap, op=ALU.mult), r=qc_t.keys + eq.keys, w=qi.keys)
                    pW = bank_bf(4)
                    for h in range(4):
                        P.pe(lambda e, h=h: e.transpose(pW.ap[:, h * 128:(h + 1) * 128], knT.ap[:, h, :], ident.ap), r=knT.keys + ident.keys, w=pW.keys)
                    P.act(lambda e: e.activation(func=AF.Identity, out=kn.ap.rearrange("p h d -> p (h d)"), in_=pW.ap[:, 0:512]), r=pW.keys, w=kn.keys)
                    if full:
                        pZ = bank(3)
                        for h in range(4):
                            P.pe(lambda e, h=h: e.matmul(pZ.ap[:, h * 128:(h + 1) * 128], lhsT=knT.ap[:, h, :], rhs=qi.ap[:, h, :], start=True, stop=True),
                                 r=knT.keys + qi.keys, w=pZ.keys)
                        P.dve(lambda e: e.tensor_tensor(out=am.ap.rearrange("p h t -> p (h t)"), in0=pZ.ap, in1=masks[dirn].ap.rearrange("p h t -> p (h t)"), op=ALU.mult),
                              r=pZ.keys + masks[dirn].keys, w=am.keys)
                        pO = [bank(5), bank(6)]
                        for h in range(4):
                            po = pO[h // 2]
                            col = (h % 2) * 129
                            P.pe(lambda e, h=h, po=po, col=col: e.matmul(po.ap[:, col:col + 129], lhsT=am.ap[:, h, :], rhs=vc_t.ap[:, h, :], start=True, stop=False),
                                 r=am.keys + vc_t.keys, w=po.keys)
                            P.pe(lambda e, h=h, po=po, col=col: e.matmul(po.ap[:, col:col + 129], lhsT=qi.ap[:, h, :], rhs=Sbf.ap[:, h, :], start=False, stop=True),
                                 r=qi.keys + Sbf.keys, w=po.keys)
                    pD = [bank(0), bank(1)]
                    for h in range(4):
                        pd = pD[h // 2]
                        col = (h % 2) * 129
                        P.pe(lambda e, h=h, pd=pd, col=col: e.matmul(pd.ap[:, col:col + 129], lhsT=kn.ap[:, h, :], rhs=vc_t.ap[:, h, :], start=True, stop=True),
                             r=kn.keys + vc_t.keys, w=pd.keys)
                    for b2 in range(2):
                        P.dve(lambda e, b2=b2: e.tensor_tensor(out=S.ap[:, 2 * b2:2 * b2 + 2, :], in0=S.ap[:, 2 * b2:2 * b2 + 2, :],
                                                               in1=pD[b2].ap[:, 0:258].rearrange("p (h v) -> p h v", h=2), op=ALU.add),
                              r=S.keys + pD[b2].keys, w=S.keys)
                    P.dve(lambda e: e.tensor_tensor(out=S.ap, in0=S.ap, in1=el.ap.unsqueeze(2).to_broadcast([128, 4, 129]), op=ALU.mult), r=S.keys + el.keys, w=S.keys)
                    P.act(lambda e: e.activation(func=AF.Identity, out=Sbf.ap, in_=S.ap), r=S.keys, w=Sbf.keys)
                    if not full:
                        return
                    od = scr["oacc" + m]

                    def copy_po():
                        for b2 in range(2):
                            P.act(lambda e, b2=b2: e.activation(func=AF.Identity, out=ot.ap[:, 2 * b2:2 * b2 + 2, :], in_=pO[b2].ap[:, 0:258].rearrange("p (h v) -> p h v", h=2)),
                                  r=pO[b2].keys, w=ot.keys)

                    def mlnorm(dst, dkeys):
                        P.dve(lambda e: e.scalar_tensor_tensor(out=s8.ap[:, 8:12], in0=ot.ap[:, :, 128], scalar=-1.0, in1=ot.ap[:, :, 128], op0=ALU.mult, op1=ALU.max),
                              r=ot.keys, w=s8.keys)
                        P.dve(lambda e: e.tensor_scalar_max(out=s8.ap[:, 8:12], in0=s8.ap[:, 8:12], scalar1=1.0), r=s8.keys, w=s8.keys)
                        P.dve(lambda e: e.reciprocal(out=s8.ap[:, 12:16], in_=s8.ap[:, 8:12]), r=s8.keys, w=s8.keys)
                        P.dve(lambda e: e.tensor_tensor(out=dst, in0=ot.ap[:, :, 0:128], in1=s8.ap[:, 12:16].unsqueeze(2).to_broadcast([128, 4, 128]), op=ALU.mult),
                              r=ot.keys + s8.keys, w=dkeys)

                    if not second:
                        copy_po()
                        if ml:
                            mlnorm(ot.ap[:, :, 0:128], ot.keys)
                        P.dma('pool', lambda e: e.dma_start(out=od[tok:tok + 128, :], in_=ot.ap.rearrange("p h v -> p (h v)")), r=ot.keys, w=[(id(od), tok)])
                        return
                    if ml:
                        copy_po()
                        mlnorm(hv.ap, hv.keys)
                        P.dve(lambda e: e.tensor_tensor(out=hv.ap, in0=hv.ap, in1=oa_t.ap[:, :, 0:128], op=ALU.add), r=hv.keys + oa_t.keys, w=hv.keys)
                        src = hv.ap
                    else:
                        for b2 in range(2):
                            P.dve(lambda e, b2=b2: e.tensor_tensor(out=ot.ap[:, 2 * b2:2 * b2 + 2, :], in0=oa_t.ap[:, 2 * b2:2 * b2 + 2, :],
                                                                   in1=pO[b2].ap[:, 0:258].rearrange("p (h v) -> p h v", h=2), op=ALU.add),
                                  r=oa_t.keys + pO[b2].keys, w=ot.keys)
                        src = ot.ap[:, :, 0:128]
                    P.dve(lambda e: e.memset(s8.ap[:, 0:4], 0.0), w=s8.keys)
                    for h in range(4):
                        P.act(lambda e, h=h, src=src: e.activation(out=eq.ap[:, h, :], in_=src[:, h, :], func=AF.Square, accum_out=s8.ap[:, h:h + 1]),
                              r=ot.keys + hv.keys + s8.keys, w=eq.keys + s8.keys)
                    P.act(lambda e: e.activation(out=s8.ap[:, 4:8], in_=s8.ap[:, 0:4], func=AF.Sqrt, scale=1.0 / 128, bias=EPS), r=s8.keys, w=s8.keys)
                    P.dve(lambda e: e.reciprocal(out=s8.ap[:, 0:4], in_=s8.ap[:, 4:8]), r=s8.keys, w=s8.keys)
                    P.dve(lambda e, src=src: e.tensor_tensor(out=hv.ap, in0=src, in1=s8.ap[:, 0:4].unsqueeze(2).to_broadcast([128, 4, 128]), op=ALU.mult),
                          r=ot.keys + hv.keys + s8.keys, w=hv.keys)
                    P.dve(lambda e: e.tensor_tensor(out=hv.ap, in0=hv.ap, in1=gnb.ap, op=ALU.mult), r=hv.keys + gnb.keys, w=hv.keys)
                    P.dve(lambda e: e.tensor_tensor(out=yb.ap, in0=hv.ap.rearrange("p h v -> p (h v)"), in1=gt_t.ap, op=ALU.mult), r=hv.keys + gt_t.keys, w=yb.keys)
                    pT = bank_bf(7)
                    for h in range(4):
                        P.pe(lambda e, h=h: e.transpose(pT.ap[:, h * 128:(h + 1) * 128], yb.ap[:, h * 128:(h + 1) * 128], ident.ap), r=yb.keys + ident.keys, w=pT.keys)
                    P.act(lambda e: e.activation(func=AF.Identity, out=yt4.ap.rearrange("p g t -> p (g t)"), in_=pT.ap[:, 0:512]), r=pT.keys, w=yt4.keys)
                    P.dma('pool', lambda e: e.dma_start(out=yT[4 * mi:4 * mi + 4, :, tok:tok + 128].rearrange("k p t -> p k t"), in_=yt4.ap),
                          r=yt4.keys, w=[("yT", 4 * mi, c)])

                def zeroS():
                    P.dve(lambda e: e.memset(S.ap, 0.0), w=S.keys)
                    P.dve(lambda e: e.memset(Sbf.ap, 0.0), w=Sbf.keys)

                for dirn in range(2):
                    second = dirn == 1
                    cchunks = [0, 1] if dirn == 0 else [1, 0]
                    lchunks = list(range(2, NT)) if dirn == 0 else list(range(NT - 1, 1, -1))
                    zeroS()
                    for c in cchunks:
                        unit(c, dirn, not A, second)
                    if A:
                        P.dma('sp', lambda e, dirn=dirn: e.dma_start(out=st_ctx[mi, dirn], in_=S.ap.rearrange("p h v -> p (h v)")), r=S.keys, w=[("st_ctx", mi, dirn)])
                        zeroS()
                        P.dve(lambda e: e.memset(bt.ap, 0.0), w=bt.keys)
                    else:
                        P.dma('sp', lambda e, dirn=dirn: e.dma_start(out=S.ap.rearrange("p h v -> p (h v)"), in_=in_s0[mi, dirn]), r=S.keys, w=S.keys)
                        for i in range(3):
                            P.dma('sp', lambda e, dirn=dirn, i=i: e.dma_start(out=ot.ap.rearrange("p h v -> p (h v)"), in_=in_loc[mi, dirn, i]), r=ot.keys, w=ot.keys)
                            P.dma('sp', lambda e, dirn=dirn, i=i: e.dma_start(out=s8.ap[:, 0:4], in_=in_bt[mi, dirn, i]), r=s8.keys, w=s8.keys)
                            P.act(lambda e: e.activation(out=s8.ap[:, 4:8], in_=s8.ap[:, 0:4], func=AF.Exp), r=s8.keys, w=s8.keys)
                            P.dve(lambda e: e.tensor_tensor(out=S.ap, in0=S.ap, in1=s8.ap[:, 4:8].unsqueeze(2).to_broadcast([128, 4, 129]), op=ALU.mult),
                                  r=S.keys + s8.keys, w=S.keys)
                            P.dve(lambda e: e.tensor_tensor(out=S.ap, in0=S.ap, in1=ot.ap, op=ALU.add), r=S.keys + ot.keys, w=S.keys)
                        P.act(lambda e: e.activation(func=AF.Identity, out=Sbf.ap, in_=S.ap), r=S.keys, w=Sbf.keys)
                    for c in lchunks:
                        unit(c, dirn, not A, second)
                    if A:
                        P.dma('sp', lambda e, dirn=dirn: e.dma_start(out=st_loc[mi, dirn], in_=S.ap.rearrange("p h v -> p (h v)")), r=S.keys, w=[("st_loc", mi, dirn)])
                        P.dma('sp', lambda e, dirn=dirn: e.dma_start(out=st_bt[mi, dirn], in_=bt.ap), r=bt.keys, w=[("st_bt", mi, dirn)])

        def TGLO(tok):
            for lo, n in TG:
                if lo <= tok < lo + n:
                    return lo

        def outproj():
            ar.reset()
            yts = [ar.alloc([KC, 128], BF16) for _ in range(2)]
            xss = [ar.alloc([512], F32) for _ in range(2)]
            gl = [ar.alloc([512], F32) for _ in range(2)]
            gc = [ar.alloc([512], F32) for _ in range(2)]
            xo = [ar.alloc([512], F32) for _ in range(2)]
            for cb in range(4):
                wt = loadw([(w_out[:, cb * 512:(cb + 1) * 512], 0, 512)])
                g_l, g_c = gl[cb % 2], gc[cb % 2]
                P.dma('sp', lambda e, g_l=g_l, cb=cb: e.dma_start(out=g_l.ap, in_=modrow[0:1, 4096 + cb * 512:4096 + (cb + 1) * 512].partition_broadcast(128)),
                      r=["modrow"], w=g_l.keys)
                P.dma('sp', lambda e, g_c=g_c, cb=cb: e.dma_start(out=g_c.ap, in_=modrow[1:2, 4096 + cb * 512:4096 + (cb + 1) * 512].partition_broadcast(128)),
                      r=["modrow"], w=g_c.keys)
                for tt in range(NT):
                    yt, xs, x_o = yts[tt % 2], xss[tt % 2], xo[tt % 2]
                    P.dma('sp', lambda e, yt=yt, tt=tt: e.dma_start(out=yt.ap, in_=yT[:, :, tt * 128:(tt + 1) * 128].rearrange("k p t -> p k t")),
                          r=[("yT", k, TGLO(tt * 128)) for k in range(8, 12)] + [("yT", k, tt) for k in (0, 4, 12)], w=yt.keys)
                    P.dma('sp', lambda e, xs=xs, tt=tt, cb=cb: e.dma_start(out=xs.ap, in_=xin[tt * 128:(tt + 1) * 128, cb * 512:(cb + 1) * 512]), w=xs.keys)
                    pb = nbank(0, 4)
                    for kc in range(KC):
                        P.pe(lambda e, kc=kc, yt=yt, pb=pb, wt=wt: e.matmul(pb.ap, lhsT=yt.ap[:, kc, :], rhs=wt.ap[:, kc, :], start=(kc == 0), stop=(kc == KC - 1)),
                             r=yt.keys + wt.keys, w=pb.keys)
                    g = g_c if tt < 2 else g_l
                    P.dve(lambda e, x_o=x_o, pb=pb, g=g: e.tensor_tensor(out=x_o.ap, in0=pb.ap, in1=g.ap, op=ALU.mult), r=pb.keys + g.keys, w=x_o.keys)
                    P.dve(lambda e, x_o=x_o, xs=xs: e.tensor_tensor(out=x_o.ap, in0=x_o.ap, in1=xs.ap, op=ALU.add), r=x_o.keys + xs.keys, w=x_o.keys)
                    P.dma('pool', lambda e, x_o=x_o, tt=tt, cb=cb: e.dma_start(out=x1[tt * 128:(tt + 1) * 128, cb * 512:(cb + 1) * 512], in_=x_o.ap),
                          r=x_o.keys, w=[("x1", tt, cb)])

        def ffn():
            mv2 = modvecs(2)

            ar.reset()
            xts = [ar.alloc([D], F32), ar.alloc([D], F32)]
            xns = [ar.alloc([D], BF16), ar.alloc([D], BF16)]
            st4 = [ar.alloc([4], F32), ar.alloc([4], F32)]
            for tt in range(NT):
                xt, xn, s4 = xts[tt % 2], xns[tt % 2], st4[tt % 2]
                scale, shift = mv2[1] if tt < 2 else mv2[0]
                P.dma('sp', lambda e, xt=xt, tt=tt: e.dma_start(out=xt.ap, in_=x1[tt * 128:(tt + 1) * 128, :]), r=[("x1", tt, cb) for cb in range(4)], w=xt.keys)
                P.dve(lambda e, s4=s4: e.memset(s4.ap, 0.0), w=s4.keys)
                P.act(lambda e, xt=xt, xn=xn, s4=s4: e.activation(out=xn.ap, in_=xt.ap, func=AF.Square, accum_out=s4.ap[:, 0:1]),
                      r=xt.keys + s4.keys, w=xn.keys + s4.keys)
                P.act(lambda e, s4=s4: e.activation(out=s4.ap[:, 1:2], in_=s4.ap[:, 0:1], func=AF.Sqrt, scale=1.0 / D, bias=EPS), r=s4.keys, w=s4.keys)
                P.dve(lambda e, s4=s4: e.reciprocal(out=s4.ap[:, 2:3], in_=s4.ap[:, 1:2]), r=s4.keys, w=s4.keys)
                P.dve(lambda e, xt=xt, xn=xn, s4=s4: e.tensor_scalar(out=xn.ap, in0=xt.ap, scalar1=s4.ap[:, 2:3], scalar2=None, op0=ALU.mult),
                      r=xt.keys + s4.keys, w=xn.keys)
                for half in range(2):
                    pb = bank_bf(2 * (tt % 2) + half)
                    for k8 in range(8):
                        kc = half * 8 + k8
                        P.pe(lambda e, pb=pb, xn=xn, kc=kc, k8=k8: e.transpose(pb.ap[:, k8 * 128:(k8 + 1) * 128], xn.ap[:, kc * 128:(kc + 1) * 128], ident.ap),
                             r=xn.keys + ident.keys, w=pb.keys)
                    for k8 in range(8):
                        kc = half * 8 + k8
                        eng = P.act if k8 % 2 == 0 else P.dve
                        if k8 % 2 == 0:
                            P.act(lambda e, pb=pb, kc=kc, k8=k8, tt=tt, scale=scale, shift=shift: e.activation(
                                out=big[:, kc, tt * 128:(tt + 1) * 128], in_=pb.ap[:, k8 * 128:(k8 + 1) * 128], func=AF.Identity,
                                scale=scale.ap[:, kc:kc + 1], bias=shift.ap[:, kc:kc + 1]),
                                r=pb.keys + scale.keys + shift.keys, w=[("hT", tt)])
                        else:
                            P.dve(lambda e, pb=pb, kc=kc, k8=k8, tt=tt, scale=scale, shift=shift: e.tensor_scalar(
                                out=big[:, kc, tt * 128:(tt + 1) * 128], in0=pb.ap[:, k8 * 128:(k8 + 1) * 128],
                                scalar1=scale.ap[:, kc:kc + 1], scalar2=shift.ap[:, kc:kc + 1], op0=ALU.mult, op1=ALU.add),
                                r=pb.keys + scale.keys + shift.keys, w=[("hT", tt)])
            ar.reset()
            fcw = ar.alloc([NFB, 3], F32)
            for j in range(3):
                P.dma('sp', lambda e, j=j: e.dma_start(out=fcw.ap[:, :, j], in_=ffn_conv[j:j + 1, :].rearrange("r (b p) -> p (r b)", p=128), allow_slow_non_contiguous=True), w=fcw.keys)
            asb = [ar.alloc([512], F32) for _ in range(2)]
            cvs = [ar.alloc([512], F32) for _ in range(2)]
            ats = [ar.alloc([512], BF16) for _ in range(2)]
            ctr = 0
            for i in range(NFB // 2):
                wt = loadw([(ffn_up[:, 256 * i:256 * i + 256], 0, 256), (ffn_up[:, DFF + 256 * i:DFF + 256 * i + 256], 256, 256)])
                for tg in TG:
                    lo, n = tg
                    for j in range(2):
                        fb = 2 * i + j
                        pa = nbank(0, 8)
                        pv = nbank(0, 8)
                        fm(wt, j * 128, 128, tg, pa)
                        fm(wt, 256 + j * 128, 128, tg, pv)
                        a, c, at = asb[ctr % 2], cvs[ctr % 2], ats[ctr % 2]
                        ctr += 1
                        P.act(lambda e, a=a, pa=pa, n=n: e.activation(func=AF.Identity, out=a.ap[:, 0:n], in_=pa.ap[:, 0:n]), r=pa.keys, w=a.keys)
                        for fn in conv3(a.ap, c.ap, n, fcw.ap[:, fb, :], lo):
                            P.dve(fn, r=a.keys + c.keys + fcw.keys, w=c.keys)
                        P.act(lambda e, c=c, n=n: e.activation(out=c.ap[:, 0:n], in_=c.ap[:, 0:n], func=AF.Silu), r=c.keys, w=c.keys)
                        P.dve(lambda e, at=at, c=c, pv=pv, n=n: e.tensor_tensor(out=at.ap[:, 0:n], in0=c.ap[:, 0:n], in1=pv.ap[:, 0:n], op=ALU.mult),
                              r=c.keys + pv.keys, w=at.keys)
                        P.dma('sp', lambda e, at=at, fb=fb, lo=lo, n=n: e.dma_start(out=actT[fb, :, lo:lo + n], in_=at.ap[:, 0:n]), r=at.keys, w=[("actT", fb, lo)])
            ar.reset()
            bigf = big[:].rearrange("p a b -> p (a b)")
            wds = [V(bigf[:, k * 11264:(k + 1) * 11264].rearrange("p (k n) -> p k n", n=256), [("wd", k)]) for k in range(3)]
            atts = [ar.alloc([NFB, 128], BF16) for _ in range(2)]
            xss = [ar.alloc([256], F32) for _ in range(2)]
            gl = [ar.alloc([256], F32) for _ in range(2)]
            gc = [ar.alloc([256], F32) for _ in range(2)]
            xo = [ar.alloc([256], F32) for _ in range(2)]
            for cb in range(8):
                wd = wds[cb % 3]
                for q4 in range(4):
                    P.dma('pool', lambda e, wd=wd, cb=cb, q4=q4: e.dma_start(
                        out=wd.ap[:, 11 * q4:11 * q4 + 11, :],
                        in_=ffn_down[:, cb * 256:(cb + 1) * 256].rearrange("(kc p) n -> p kc n", p=128)[:, 11 * q4:11 * q4 + 11, :]),
                        w=wd.keys + (hkeys if cb < 3 else []))
                g_l, g_c = gl[cb % 2], gc[cb % 2]
                P.dma('sp', lambda e, g_l=g_l, cb=cb: e.dma_start(out=g_l.ap, in_=modrow[0:1, 10240 + cb * 256:10240 + (cb + 1) * 256].partition_broadcast(128)),
                      r=["modrow"], w=g_l.keys)
                P.dma('sp', lambda e, g_c=g_c, cb=cb: e.dma_start(out=g_c.ap, in_=modrow[1:2, 10240 + cb * 256:10240 + (cb + 1) * 256].partition_broadcast(128)),
                      r=["modrow"], w=g_c.keys)
                for tt in range(NT):
                    att, xs, x_o = atts[tt % 2], xss[tt % 2], xo[tt % 2]
                    P.dma('sp', lambda e, att=att, tt=tt: e.dma_start(out=att.ap, in_=actT[:, :, tt * 128:(tt + 1) * 128].rearrange("k p t -> p k t")),
                          r=[("actT", fb, TGLO(tt * 128)) for fb in range(NFB)], w=att.keys)
                    P.dma('sp', lambda e, xs=xs, tt=tt, cb=cb: e.dma_start(out=xs.ap, in_=x1[tt * 128:(tt + 1) * 128, cb * 256:(cb + 1) * 256]),
                          r=[("x1", tt, cb // 2)], w=xs.keys)
                    pb = nbank(0, 8)
                    for kc in range(NFB):
                        P.pe(lambda e, kc=kc, att=att, pb=pb, wd=wd: e.matmul(pb.ap[:, 0:256], lhsT=att.ap[:, kc, :], rhs=wd.ap[:, kc, :], start=(kc == 0), stop=(kc == NFB - 1)),
                             r=att.keys + wd.keys, w=pb.keys)
                    g = g_c if tt < 2 else g_l
                    P.dve(lambda e, x_o=x_o, pb=pb, g=g: e.tensor_tensor(out=x_o.ap, in0=pb.ap[:, 0:256], in1=g.ap, op=ALU.mult), r=pb.keys + g.keys, w=x_o.keys)
                    P.dve(lambda e, x_o=x_o, xs=xs: e.tensor_tensor(out=x_o.ap, in0=x_o.ap, in1=xs.ap, op=ALU.add), r=x_o.keys + xs.keys, w=x_o.keys)
                    P.dma('pool', lambda e, x_o=x_o, tt=tt, cb=cb: e.dma_start(out=xout[tt * 128:(tt + 1) * 128, cb * 256:(cb + 1) * 256], in_=x_o.ap),
                          r=x_o.keys, w=[("xout", tt, cb)])

        def finalnorm():
            ar.reset()
            fnb = ar.alloc([D], F32)
            P.dma('sp', lambda e: e.dma_start(out=fnb.ap, in_=final_norm.partition_broadcast(128)), w=fnb.keys)
            xts = [ar.alloc([D], F32), ar.alloc([D], F32)]
            yos = [ar.alloc([D], F32), ar.alloc([D], F32)]
            st4 = [ar.alloc([4], F32), ar.alloc([4], F32)]
            for tt in range(2, NT):
                xt, yo, s4 = xts[tt % 2], yos[tt % 2], st4[tt % 2]
                P.dma('sp', lambda e, xt=xt, tt=tt: e.dma_start(out=xt.ap, in_=xout[tt * 128:(tt + 1) * 128, :]), r=[("xout", tt, cb) for cb in range(8)], w=xt.keys)
                P.dve(lambda e, s4=s4: e.memset(s4.ap, 0.0), w=s4.keys)
                P.act(lambda e, xt=xt, yo=yo, s4=s4: e.activation(out=yo.ap, in_=xt.ap, func=AF.Square, accum_out=s4.ap[:, 0:1]),
                      r=xt.keys + s4.keys, w=yo.keys + s4.keys)
                P.act(lambda e, s4=s4: e.activation(out=s4.ap[:, 1:2], in_=s4.ap[:, 0:1], func=AF.Sqrt, scale=1.0 / D, bias=EPS), r=s4.keys, w=s4.keys)
                P.dve(lambda e, s4=s4: e.reciprocal(out=s4.ap[:, 2:3], in_=s4.ap[:, 1:2]), r=s4.keys, w=s4.keys)
                P.dve(lambda e, xt=xt, yo=yo, s4=s4: e.scalar_tensor_tensor(out=yo.ap, in0=xt.ap, scalar=s4.ap[:, 2:3], in1=fnb.ap, op0=ALU.mult, op1=ALU.mult),
                      r=xt.keys + s4.keys + fnb.keys, w=yo.keys)
                P.dma('pool', lambda e, yo=yo, tt=tt: e.dma_start(out=yfinal[(tt - 2) * 128:(tt - 1) * 128, :], in_=yo.ap), r=yo.keys, w=[("yfinal", tt)])

        mv1 = modvecs(1)
        build_hT(xin, mv1)
        inproj()
        scans()
        if not A:
            outproj()
            ffn()
            finalnorm()
        P.finalize(st)
        print(mode, P.stats, flush=True)
    return nc


_CACHE = {}


def _prog(mode):
    if mode not in _CACHE:
        _CACHE[mode] = build(mode)
    return _CACHE[mode]


def _run_layer(l, xs, c, c_ctx, W, final_norm):
    f = lambda a: np.ascontiguousarray(np.asarray(a, dtype=np.float32))
    ncore = 8
    progA = _prog('A')
    progB = _prog('B')
    common = {
        "norm1": f(W["norm1"][l]).reshape(1, D), "w_in": f(W["w_in"][l]), "gla_wa2": f(W["gla_wa2"][l]), "gla_ba": f(W["gla_ba"][l]),
        "mlstm_conv": f(W["mlstm_conv"][l]), "mlstm_gate_b": f(W["mlstm_gate_b"][l]).reshape(1, 16),
    }
    insA = []
    for core in range(ncore):
        b = core // 4
        d = dict(common)
        d["xin"] = xs[core]
        d["c2"] = np.stack([c[b], c_ctx], axis=0)
        d["w_mod"] = f(W["w_mod"][l])
        d["b_mod"] = f(W["b_mod"][l]).reshape(1, 12288)
        insA.append(d)
    resA = run_bass_kernel_spmd(progA, insA, core_ids=list(range(ncore))).results
    extra = {
        "norm2": f(W["norm2"][l]).reshape(1, D), "w_out": f(W["w_out"][l]), "gla_norm": f(W["gla_norm"][l]).reshape(1, 512),
        "mlstm_norm": f(W["mlstm_norm"][l]).reshape(1, 512), "sc_conv": f(W["sc_conv"][l]), "cm_ws": f(W["cm_ws"][l]), "cm_bs": f(W["cm_bs"][l]),
        "cm_norm": f(W["cm_norm"][l]).reshape(1, 512), "ffn_up": f(W["ffn_up"][l]), "ffn_conv": f(W["ffn_conv"][l]), "ffn_down": f(W["ffn_down"][l]),
        "final_norm": f(final_norm).reshape(1, D),
    }
    insB = []
    for core in range(ncore):
        b, j = core // 4, core % 4
        d = dict(common)
        d.update(extra)
        d["xin"] = xs[core]
        d["modrow"] = resA[core]["modrow"]
        d["in_s0"] = resA[core]["st_ctx"]
        loc = np.zeros((2, 2, 3, 128, 516), np.float32)
        bt = np.zeros((2, 2, 3, 128, 4), np.float32)
        for mi in range(2):
            for i in range(3):
                if i < j:
                    loc[mi, 0, i] = resA[4 * b + i]["st_loc"][mi, 0]
                    bt[mi, 0, i] = resA[4 * b + i]["st_bt"][mi, 0]
                seg = 3 - i
                if seg > j:
                    loc[mi, 1, i] = resA[4 * b + seg]["st_loc"][mi, 1]
                    bt[mi, 1, i] = resA[4 * b + seg]["st_bt"][mi, 1]
        d["in_loc"] = loc
        d["in_bt"] = bt
        insB.append(d)
    resB = run_bass_kernel_spmd(progB, insB, core_ids=list(range(ncore))).results
    return resA, resB


def kernel(x, c, ctx, c_ctx, w_mod, b_mod, norm1, norm2, w_in, w_out, gla_wa2, gla_ba, gla_norm,
           mlstm_conv, mlstm_gate_b, mlstm_norm, sc_conv, cm_ws, cm_bs, cm_norm,
           ffn_up, ffn_conv, ffn_down, final_norm):
    f = lambda a: np.ascontiguousarray(np.asarray(a, dtype=np.float32))
    x, c, ctx, c_ctx = f(x), f(c), f(ctx), f(c_ctx)
    W = dict(w_mod=w_mod, b_mod=b_mod, norm1=norm1, norm2=norm2, w_in=w_in, w_out=w_out, gla_wa2=gla_wa2, gla_ba=gla_ba,
             gla_norm=gla_norm, mlstm_conv=mlstm_conv, mlstm_gate_b=mlstm_gate_b, mlstm_norm=mlstm_norm, sc_conv=sc_conv,
             cm_ws=cm_ws, cm_bs=cm_bs, cm_norm=cm_norm, ffn_up=ffn_up, ffn_conv=ffn_conv, ffn_down=ffn_down)
    xs = []
    for core in range(8):
        b, j = core // 4, core % 4
        xs.append(np.concatenate([ctx[b], x[b, j * 2048:(j + 1) * 2048]], axis=0))
    out = np.zeros((2, 8192, D), np.float32)
    for l in range(2):
        resA, resB = _run_layer(l, xs, c, c_ctx, W, final_norm)
        for core in range(8):
            xs[core] = resB[core]["xout"]
            if l == 1:
                b, j = core // 4, core % 4
                out[b, j * 2048:(j + 1) * 2048] = resB[core]["yfinal"]
    return out
```

```python
import contextlib
import math
import numpy as np
import concourse.bass as bass
import concourse.mybir as mybir
from concourse.bass_utils import run_bass_kernel_spmd

F32 = mybir.dt.float32
BF16 = mybir.dt.bfloat16
AF = mybir.ActivationFunctionType
ALU = mybir.AluOpType

D = 2048
KC = 16
T = 2304
NT = 18
HD = 128
DFF = 5632
NFB = 44
OFF_GLA = 0
OFF_ML = 2080
OFF_SC = 4144
OFF_CM = 5680
TG = [(0, 256), (256, 512), (768, 512), (1280, 512), (1792, 512)]
EPS = 1e-6
NDMA_SEM = 6
import os as _os
NOCC = bool(_os.environ.get('MK_NOCC'))
QS = HD ** -0.5


class Prog:
    def __init__(self, nc, same_engine_sync=True):
        self.nc = nc
        self.ops = []
        self.same_engine_sync = same_engine_sync

    def op(self, eng, fn, reads=(), writes=(), dma=False):
        self.ops.append((eng, fn, tuple(reads), tuple(writes), dma))

    def pe(self, fn, r=(), w=()):
        self.op('pe', fn, r, w)

    def act(self, fn, r=(), w=()):
        self.op('act', fn, r, w)

    def dve(self, fn, r=(), w=()):
        self.op('dve', fn, r, w)

    def pool(self, fn, r=(), w=()):
        self.op('pool', fn, r, w)

    def dma(self, q, fn, r=(), w=()):
        self.op(q, fn, r, w, dma=True)

    def finalize(self, stack):
        nc = self.nc
        ops = self.ops
        n = len(ops)
        dma_ctr = {}
        dom = [None] * n
        pos = [0] * n
        dom_len = {}
        extra_dep = [None] * n
        last_in_dom = {}
        for i, (eng, fn, r, w, dma) in enumerate(ops):
            if dma == 'cc':
                k = dma_ctr.get('cc', 0)
                dma_ctr['cc'] = k + 1
                d = ('cc', k)
            elif dma:
                k = dma_ctr.get(eng, 0)
                dma_ctr[eng] = k + 1
                d = (eng, k % NDMA_SEM)
            else:
                d = eng
            dom[i] = d
            p = dom_len.get(d, 0) + 1
            dom_len[d] = p
            pos[i] = p
            if dma and d in last_in_dom:
                extra_dep[i] = last_in_dom[d]
            last_in_dom[d] = i
        last_w = {}
        readers = {}
        cur_vc = {}
        vc = [None] * n
        waits = [None] * n
        marked = [False] * n
        for i, (eng, fn, r, w, dma) in enumerate(ops):
            deps = set()
            for res in r:
                j = last_w.get(res)
                if j is not None:
                    deps.add(j)
            for res in w:
                j = last_w.get(res)
                if j is not None:
                    deps.add(j)
                rs = readers.get(res)
                if rs:
                    deps.update(rs)
            if extra_dep[i] is not None:
                deps.add(extra_dep[i])
            cv = cur_vc.setdefault(eng, {})
            my_waits = []
            for j in sorted(deps, reverse=True):
                dj = dom[j]
                if dj == dom[i] and not dma:
                    if eng == 'pe' or not self.same_engine_sync:
                        continue
                if cv.get(dj, 0) >= pos[j]:
                    continue
                my_waits.append(j)
                marked[j] = True
                for kk, vv in vc[j].items():
                    if cv.get(kk, 0) < vv:
                        cv[kk] = vv
            waits[i] = my_waits
            v = dict(cv)
            if v.get(dom[i], 0) < pos[i]:
                v[dom[i]] = pos[i]
            vc[i] = v
            for res in r:
                readers.setdefault(res, set()).add(i)
            for res in w:
                last_w[res] = i
                readers[res] = set()
        count = [0] * n
        cnt = {}
        for i in range(n):
            d = dom[i]
            if ops[i][4]:
                marked[i] = True
            if marked[i]:
                c = cnt.get(d, 0) + (16 if ops[i][4] is True else 1)
                cnt[d] = c
                count[i] = c
        sems = {}
        for d in dom_len:
            nm = d if isinstance(d, str) else f"{d[0]}{d[1]}"
            sems[d] = stack.enter_context(nc.semaphore("s_" + nm))
        self.stats = dict(n_ops=n, n_waits=sum(len(x) for x in waits), n_marked=sum(marked),
                          max_count=max(cnt.values()) if cnt else 0)
        by_eng = {}
        for i in range(n):
            by_eng.setdefault(ops[i][0], []).append(i)
        block = stack.enter_context(nc.Block())
        engmap = {'pe': 'tensor', 'act': 'scalar', 'dve': 'vector', 'pool': 'gpsimd', 'sp': 'sync'}

        def make(e, idxs):
            def body(engine):
                for i in idxs:
                    wd = {}
                    for j in waits[i]:
                        dj = dom[j]
                        if wd.get(dj, 0) < count[j]:
                            wd[dj] = count[j]
                    for dj, cval in wd.items():
                        engine.wait_ge(sems[dj], cval)
                    ins = ops[i][1](engine)
                    if marked[i]:
                        if ops[i][4] == 'cc':
                            ins.then_inc(sems[dom[i]])
                        else:
                            ins.then_inc(sems[dom[i]], 16 if ops[i][4] else 1)
                for d, c in cnt.items():
                    if not isinstance(d, str) and d[0] == e:
                        engine.wait_ge(sems[d], c)
            return body

        for e, idxs in by_eng.items():
            getattr(block, engmap[e])(make(e, idxs))


class V:
    def __init__(self, ap, keys):
        self.ap = ap
        self.keys = keys


class Arena:
    def __init__(self, t, nbytes, name):
        self.t = t
        self.n = nbytes
        self.p = 0
        self.name = name

    def reset(self):
        self.p = 0

    def alloc(self, shape, dt):
        esz = 4 if dt == F32 else 2
        nel = 1
        for s in shape:
            nel *= s
        nb = nel * esz
        off = (self.p + 63) // 64 * 64
        assert off + nb <= self.n, (self.name, off, nb, self.n)
        self.p = off + nb
        ap = self.t[:, off // 2:(off + nb) // 2]
        if dt == F32:
            ap = ap.bitcast(F32)
        if len(shape) > 1:
            names = [f"d{i}" for i in range(len(shape))]
            ap = ap.rearrange("p (" + " ".join(names) + ") -> p " + " ".join(names),
                              **{nm: s for nm, s in zip(names, shape)})
        keys = [(self.name, k) for k in range(off // 512, (off + nb - 1) // 512 + 1)]
        return V(ap, keys)


WNAMES = [("w_mod", [D, 12288]), ("b_mod", [1, 12288]), ("norm1", [1, D]), ("norm2", [1, D]), ("w_in", [D, 6704]), ("w_out", [D, D]),
          ("gla_wa2", [2, 16, 512]), ("gla_ba", [2, 512]), ("gla_norm", [1, 512]), ("mlstm_conv", [3, 1024]), ("mlstm_gate_b", [1, 16]),
          ("mlstm_norm", [1, 512]), ("sc_conv", [3, 512]), ("cm_ws", [4, 128, 128]), ("cm_bs", [4, 128]), ("cm_norm", [1, 512]),
          ("ffn_up", [D, 2 * DFF]), ("ffn_conv", [3, DFF]), ("ffn_down", [DFF, D])]


def build():
    nc = bass.Bass("TRN2", target_bir_lowering=False)

    def din(name, shape, dt=F32):
        return nc.dram_tensor(name, shape, dt, kind="ExternalInput").ap()

    def dout(name, shape, dt=F32):
        return nc.dram_tensor(name, shape, dt, kind="ExternalOutput").ap()

    def dscr(name, shape, dt):
        return nc.dram_tensor(name, shape, dt).ap()

    xin = din("xin", [T, D])
    c2 = din("c2", [2, D])
    sel = din("sel", [1, 16])
    final_norm = din("final_norm", [1, D])
    IN = []
    for l in range(2):
        IN.append({nm: din(nm + str(l), list(shp)) for nm, shp in WNAMES})
    yfinal = dout("yfinal", [2048, D])
    modrow = dscr("modrow", [2, 12288], F32)
    x1 = dscr("x1", [T, D], F32)
    xlay = [dscr("xlay0", [T, D], F32), dscr("xlay1", [T, D], F32)]
    yT = dscr("yT", [16, 128, T], BF16)
    actT = dscr("actT", [NFB, 128, T], BF16)
    cc_in = [[dscr("cc_in%d_%d" % (l_, m_), [128, 1040], F32) for m_ in range(2)] for l_ in range(2)]
    cc_out = [[dscr("cc_out%d_%d" % (l_, m_), [512, 1040], F32) for m_ in range(2)] for l_ in range(2)]
    scr = {}
    for m in ("g", "m"):
        scr["kT" + m] = dscr("kT" + m, [128, 4, T], BF16)
        scr["v" + m] = dscr("v" + m, [T, 512], BF16)
        scr["qT" + m] = dscr("qT" + m, [128, 4, T], BF16)
        scr["gate" + m] = dscr("gate" + m, [T, 512], BF16)
        scr["oacc" + m] = dscr("oacc" + m, [T, 516], F32)
    zT = dscr("zT", [32, T], F32)
    gatesd = dscr("gatesd", [T, 16], F32)
    A = False
    st = contextlib.ExitStack()
    with st:
        def sb(name, shape, dt):
            return st.enter_context(nc.sbuf_tensor(name, shape, dt))

        P = Prog(nc)
        big = sb("big", [128, KC, T], BF16)
        wbuf = sb("wbuf", [128, 2, KC, 512], BF16)
        ar_t = sb("ar", [128, 22528], BF16)
        ar = Arena(ar_t, 45056, "ar")
        cn_t = sb("cn", [128, 12288], BF16)
        cn = Arena(cn_t, 24576, "cn")
        ps = st.enter_context(nc.psum_tensor("ps", [128, 8, 512], F32))

        def bank(i):
            return V(ps[:, i, :], [("ps", i)])

        def bank_bf(i):
            return V(ps[:, i, :].bitcast(BF16), [("ps", i)])

        hkeys = [("hT", tt) for tt in range(NT)]

        identf = cn.alloc([128], F32)
        ident = cn.alloc([128], BF16)
        tri = [cn.alloc([128], F32), cn.alloc([128], F32)]
        P.pool(lambda e: e.memset(identf.ap, 1.0), w=identf.keys)
        P.pool(lambda e: e.affine_select(out=identf.ap, in_=identf.ap, pattern=[[-1, 128]], compare_op=ALU.is_equal,
                                         fill=0.0, base=0, channel_multiplier=1), r=identf.keys, w=identf.keys)
        P.dve(lambda e: e.tensor_copy(out=ident.ap, in_=identf.ap), r=identf.keys, w=ident.keys)
        for k, sgn in enumerate((-1, 1)):
            P.pool(lambda e, k=k: e.memset(tri[k].ap, 1.0), w=tri[k].keys)
            P.pool(lambda e, k=k, sgn=sgn: e.affine_select(out=tri[k].ap, in_=tri[k].ap, pattern=[[-sgn, 128]], compare_op=ALU.is_ge,
                                                            fill=0.0, base=0, channel_multiplier=sgn), r=tri[k].keys, w=tri[k].keys)
        U = {}
        for m, cval in (("g", -1.0 / 16.0), ("m", -1.0)):
            for dirn in range(2):
                uq = cn.alloc([128], F32)
                uk = cn.alloc([128], F32)
                P.dve(lambda e, uq=uq, dirn=dirn, cval=cval: e.tensor_scalar(out=uq.ap, in0=tri[dirn].ap, scalar1=cval, scalar2=None, op0=ALU.mult),
                      r=tri[dirn].keys, w=uq.keys)
                P.dve(lambda e, uk=uk, dirn=dirn, cval=cval: e.tensor_scalar(out=uk.ap, in0=tri[dirn].ap, scalar1=-cval, scalar2=None, op0=ALU.mult),
                      r=tri[dirn].keys, w=uk.keys)
                U[(m, dirn)] = (uq, uk)
        masks = []
        if not A:
            for dirn in range(2):
                mk = cn.alloc([4, 128], BF16)
                for h in range(4):
                    P.dve(lambda e, mk=mk, h=h, dirn=dirn: e.tensor_copy(out=mk.ap[:, h, :], in_=tri[dirn].ap), r=tri[dirn].keys, w=mk.keys)
                masks.append(mk)
        selt = cn.alloc([16], F32)
        P.dma('sp', lambda e: e.dma_start(out=selt.ap, in_=sel.partition_broadcast(128)), w=selt.keys)

        def TGLO(tok):
            for lo, n in TG:
                if lo <= tok < lo + n:
                    return lo

        def emit_layer(l):
            W = IN[l]
            w_mod, b_mod, norm1, norm2, w_in, w_out = W["w_mod"], W["b_mod"], W["norm1"], W["norm2"], W["w_in"], W["w_out"]
            gla_wa2, gla_ba, gla_norm, mlstm_conv, mlstm_gate_b, mlstm_norm = W["gla_wa2"], W["gla_ba"], W["gla_norm"], W["mlstm_conv"], W["mlstm_gate_b"], W["mlstm_norm"]
            sc_conv, cm_ws, cm_bs, cm_norm, ffn_up, ffn_conv, ffn_down = W["sc_conv"], W["cm_ws"], W["cm_bs"], W["cm_norm"], W["ffn_up"], W["ffn_conv"], W["ffn_down"]
            xsrc = xin if l == 0 else xlay[0]
            xdst = xlay[l]

            def xkeys(tt):
                return [] if l == 0 else [("xo", l - 1, tt, cb) for cb in range(8)]
            Wa = []
            for dirn in range(2):
                wa = cn.alloc([512], F32)
                P.dve(lambda e, wa=wa: e.memset(wa.ap[0:32, :], 0.0), w=wa.keys)
                P.dma('sp', lambda e, wa=wa, dirn=dirn: e.dma_start(out=wa.ap[16 * dirn:16 * dirn + 16, :], in_=gla_wa2[dirn]), r=wa.keys, w=wa.keys)
                P.dma('sp', lambda e, wa=wa, dirn=dirn: e.dma_start(out=wa.ap[32:33, :], in_=gla_ba[dirn:dirn + 1, :]), r=wa.keys, w=wa.keys)
                Wa.append(wa)
            gb = cn.alloc([16], F32)
            P.dma('sp', lambda e: e.dma_start(out=gb.ap, in_=mlstm_gate_b.partition_broadcast(128)), w=gb.keys)
            mcw = cn.alloc([8, 3], F32)
            for j in range(3):
                P.dma('sp', lambda e, j=j: e.dma_start(out=mcw.ap[:, :, j], in_=mlstm_conv[j:j + 1, :].rearrange("r (b p) -> p (r b)", p=128), allow_slow_non_contiguous=True), w=mcw.keys)
            vec = {}

            def fmvec(name, src_row_ap):
                v = cn.alloc([KC], F32)
                P.dma('sp', lambda e: e.dma_start(out=v.ap, in_=src_row_ap.rearrange("r (kc p) -> p (r kc)", p=128), allow_slow_non_contiguous=True),
                      r=["modrow"], w=v.keys)
                vec[name] = v
                return v

            if True:
                ar.reset()
                cT = ar.alloc([KC, 2], F32)
                cTb = ar.alloc([KC, 2], BF16)
                for r in range(2):
                    P.dma('sp', lambda e, r=r: e.dma_start(out=cT.ap[:, :, r], in_=c2[r:r + 1, :].rearrange("r (kc p) -> p (r kc)", p=128), allow_slow_non_contiguous=True), w=cT.keys)
                P.act(lambda e: e.activation(out=cTb.ap, in_=cT.ap, func=AF.Silu), r=cT.keys, w=cTb.keys)
                mst = [ar.alloc([512], F32), ar.alloc([512], F32)]
                bmt = [ar.alloc([512], F32), ar.alloc([512], F32)]
                for ci in range(24):
                    wt = V(wbuf[:, ci % 2], [("wbuf", ci % 2)])
                    for half in range(2):
                        P.dma('pool', lambda e, wt=wt, ci=ci, half=half: e.dma_start(
                            out=wt.ap[:, 8 * half:8 * half + 8, :],
                            in_=w_mod[:, ci * 512:(ci + 1) * 512].rearrange("(kc p) n -> p kc n", p=128)[:, 8 * half:8 * half + 8, :]),
                            w=wt.keys)
                    bm = bmt[ci % 2]
                    P.dma('sp', lambda e, bm=bm, ci=ci: e.dma_start(out=bm.ap[0:2, :], in_=b_mod[:, ci * 512:(ci + 1) * 512].partition_broadcast(2)), w=bm.keys)
                    pb = bank(ci % 2)
                    for kc in range(KC):
                        P.pe(lambda e, pb=pb, wt=wt, kc=kc: e.matmul(pb.ap[0:2, :], lhsT=cTb.ap[:, kc, :], rhs=wt.ap[:, kc, :], start=(kc == 0), stop=(kc == KC - 1)),
                             r=cTb.keys + wt.keys, w=pb.keys)
                    ms = mst[ci % 2]
                    P.dve(lambda e, ms=ms, pb=pb, bm=bm: e.tensor_tensor(out=ms.ap[0:2, :], in0=pb.ap[0:2, :], in1=bm.ap[0:2, :], op=ALU.add),
                          r=pb.keys + bm.keys, w=ms.keys)
                    P.dma('sp', lambda e, ms=ms, ci=ci: e.dma_start(out=modrow[:, ci * 512:(ci + 1) * 512], in_=ms.ap[0:2, :]), r=ms.keys, w=["modrow"])

            n1T = cn.alloc([KC], F32)
            P.dma('sp', lambda e: e.dma_start(out=n1T.ap, in_=norm1.rearrange("r (kc p) -> p (r kc)", p=128), allow_slow_non_contiguous=True), w=n1T.keys)
            if not A:
                n2T = cn.alloc([KC], F32)
                P.dma('sp', lambda e: e.dma_start(out=n2T.ap, in_=norm2.rearrange("r (kc p) -> p (r kc)", p=128), allow_slow_non_contiguous=True), w=n2T.keys)

            def modvecs(which):
                nT = n1T if which == 1 else n2T
                off_sh = 0 if which == 1 else 6144
                off_sc = 2048 if which == 1 else 8192
                out = []
                for r in range(2):
                    sc = fmvec(f"sc{which}{r}", modrow[r:r + 1, off_sc:off_sc + D])
                    sh = fmvec(f"sh{which}{r}", modrow[r:r + 1, off_sh:off_sh + D])
                    P.dve(lambda e, sc=sc, nT=nT: e.scalar_tensor_tensor(out=sc.ap, in0=sc.ap, scalar=1.0, in1=nT.ap, op0=ALU.add, op1=ALU.mult),
                          r=sc.keys + nT.keys, w=sc.keys)
                    out.append((sc, sh))
                return out

            def build_hT(src, mv):
                ar.reset()
                xts = [ar.alloc([D], F32), ar.alloc([D], F32)]
                xns = [ar.alloc([D], BF16), ar.alloc([D], BF16)]
                st4 = [ar.alloc([4], F32), ar.alloc([4], F32)]
                for tt in range(NT):
                    xt = xts[tt % 2]
                    xn = xns[tt % 2]
                    s4 = st4[tt % 2]
                    scale, shift = mv[1] if tt < 2 else mv[0]
                    P.dma('sp', lambda e, xt=xt, tt=tt: e.dma_start(out=xt.ap, in_=src[tt * 128:(tt + 1) * 128, :]), r=xkeys(tt), w=xt.keys)
                    P.dve(lambda e, s4=s4: e.memset(s4.ap, 0.0), w=s4.keys)
                    P.act(lambda e, xt=xt, xn=xn, s4=s4: e.activation(out=xn.ap, in_=xt.ap, func=AF.Square, accum_out=s4.ap[:, 0:1]),
                          r=xt.keys + s4.keys, w=xn.keys + s4.keys)
                    P.act(lambda e, s4=s4: e.activation(out=s4.ap[:, 1:2], in_=s4.ap[:, 0:1], func=AF.Sqrt, scale=1.0 / D, bias=EPS), r=s4.keys, w=s4.keys)
                    P.dve(lambda e, s4=s4: e.reciprocal(out=s4.ap[:, 2:3], in_=s4.ap[:, 1:2]), r=s4.keys, w=s4.keys)
                    P.dve(lambda e, xt=xt, xn=xn, s4=s4: e.tensor_scalar(out=xn.ap, in0=xt.ap, scalar1=s4.ap[:, 2:3], scalar2=None, op0=ALU.mult),
                          r=xt.keys + s4.keys, w=xn.keys)
                    for half in range(2):
                        pb = bank_bf(2 * (tt % 2) + half)
                        for k8 in range(8):
                            kc = half * 8 + k8
                            P.pe(lambda e, pb=pb, xn=xn, kc=kc, k8=k8: e.transpose(pb.ap[:, k8 * 128:(k8 + 1) * 128], xn.ap[:, kc * 128:(kc + 1) * 128], ident.ap),
                                 r=xn.keys + ident.keys, w=pb.keys)
                        for k8 in range(8):
                            kc = half * 8 + k8
                            if k8 % 2 == 0:
                                P.act(lambda e, pb=pb, kc=kc, k8=k8, tt=tt, scale=scale, shift=shift: e.activation(
                                    out=big[:, kc, tt * 128:(tt + 1) * 128], in_=pb.ap[:, k8 * 128:(k8 + 1) * 128], func=AF.Identity,
                                    scale=scale.ap[:, kc:kc + 1], bias=shift.ap[:, kc:kc + 1]),
                                    r=pb.keys + scale.keys + shift.keys, w=[("hT", tt)])
                            else:
                                P.dve(lambda e, pb=pb, kc=kc, k8=k8, tt=tt, scale=scale, shift=shift: e.tensor_scalar(
                                    out=big[:, kc, tt * 128:(tt + 1) * 128], in0=pb.ap[:, k8 * 128:(k8 + 1) * 128],
                                    scalar1=scale.ap[:, kc:kc + 1], scalar2=shift.ap[:, kc:kc + 1], op0=ALU.mult, op1=ALU.add),
                                    r=pb.keys + scale.keys + shift.keys, w=[("hT", tt)])

            wctr = [0]

            def loadw(srcs, nkc=KC):
                i = wctr[0] % 2
                wctr[0] += 1
                wt = V(wbuf[:, i], [("wbuf", i)])
                for (src, off, n) in srcs:
                    for half in range(2):
                        P.dma('pool', lambda e, wt=wt, src=src, off=off, n=n, half=half: e.dma_start(
                            out=wt.ap[:, 8 * half:8 * half + 8, off:off + n],
                            in_=src.rearrange("(kc p) n -> p kc n", p=128)[:, 8 * half:8 * half + 8, :]), w=wt.keys)
                return wt

            def fm(wt, coff, m, tg, pb):
                lo, n = tg
                tts = list(range(lo // 128, (lo + n) // 128))
                for kc in range(KC):
                    P.pe(lambda e, kc=kc: e.matmul(pb.ap[0:m, 0:n], lhsT=wt.ap[:, kc, coff:coff + m], rhs=big[:, kc, lo:lo + n],
                                                   start=(kc == 0), stop=(kc == KC - 1)),
                         r=wt.keys + [("hT", t) for t in tts], w=pb.keys)

            def tm(wt, coff, ncol, tt, pb):
                for kc in range(KC):
                    P.pe(lambda e, kc=kc: e.matmul(pb.ap[:, 0:ncol], lhsT=big[:, kc, tt * 128:(tt + 1) * 128], rhs=wt.ap[:, kc, coff:coff + ncol],
                                                   start=(kc == 0), stop=(kc == KC - 1)),
                         r=wt.keys + [("hT", tt)], w=pb.keys)

            pctr = [0]

            def nbank(lo=0, hi=8):
                i = lo + pctr[0] % (hi - lo)
                pctr[0] += 1
                return bank(i)

            def conv3(src, dst, n, cw, lo):
                W = 256 if lo == 0 else 64
                R = n // W
                s3 = src[:, 0:n].rearrange("p (r w) -> p r w", w=W)
                d3 = dst[:, 0:n].rearrange("p (r w) -> p r w", w=W)
                return [
                    lambda e: e.tensor_scalar(out=dst[:, 0:n], in0=src[:, 0:n], scalar1=cw[:, 1:2], scalar2=None, op0=ALU.mult),
                    lambda e: e.scalar_tensor_tensor(out=d3[:, :, 1:W], in0=s3[:, :, 0:W - 1], scalar=cw[:, 0:1], in1=d3[:, :, 1:W], op0=ALU.mult, op1=ALU.add),
                    lambda e: e.scalar_tensor_tensor(out=d3[:, :, 0:W - 1], in0=s3[:, :, 1:W], scalar=cw[:, 2:3], in1=d3[:, :, 0:W - 1], op0=ALU.mult, op1=ALU.add),
                ]

            def inproj():
                ar.reset()
                stg = [ar.alloc([512], BF16) for _ in range(3)]
                stf = [ar.alloc([512], F32) for _ in range(2)]
                cvf = [ar.alloc([512], F32) for _ in range(2)]
                sctr = [0]

                def nstg():
                    s = stg[sctr[0] % 3]
                    sctr[0] += 1
                    return s

                def fm_to_scratch(wt, nblk, dst, scale=None, conv=None):
                    for hb in range(nblk):
                        for gi, tg in enumerate(TG):
                            lo, n = tg
                            pb = nbank(0, 4)
                            fm(wt, hb * 128, 128, tg, pb)
                            s = nstg()
                            if conv is None:
                                if scale is None:
                                    P.act(lambda e, s=s, pb=pb, n=n: e.activation(func=AF.Identity, out=s.ap[:, 0:n], in_=pb.ap[:, 0:n]), r=pb.keys, w=s.keys)
                                else:
                                    P.act(lambda e, s=s, pb=pb, n=n: e.mul(out=s.ap[:, 0:n], in_=pb.ap[:, 0:n], mul=scale), r=pb.keys, w=s.keys)
                            else:
                                a = stf[sctr[0] % 2]
                                c = cvf[sctr[0] % 2]
                                P.act(lambda e, a=a, pb=pb, n=n: e.mul(out=a.ap[:, 0:n], in_=pb.ap[:, 0:n], mul=(scale or 1.0)), r=pb.keys, w=a.keys)
                                fns = conv3(a.ap, c.ap, n, conv[:, hb, :], lo)
                                for fn in fns:
                                    P.dve(fn, r=a.keys + c.keys + mcw.keys, w=c.keys)
                                P.act(lambda e, s=s, c=c, n=n: e.activation(func=AF.Identity, out=s.ap[:, 0:n], in_=c.ap[:, 0:n]), r=c.keys, w=s.keys)
                            P.dma('sp', lambda e, s=s, hb=hb, lo=lo, n=n: e.dma_start(out=dst[:, hb, lo:lo + n], in_=s.ap[:, 0:n]), r=s.keys, w=[(id(dst), lo)])

                def tm_to_scratch(wt, dst, func=None):
                    for tt in range(NT):
                        pb = nbank(0, 4)
                        tm(wt, 0, 512, tt, pb)
                        s = nstg()
                        if func is None:
                            if tt % 2 == 0:
                                P.act(lambda e, s=s, pb=pb: e.activation(func=AF.Identity, out=s.ap, in_=pb.ap), r=pb.keys, w=s.keys)
                            else:
                                P.dve(lambda e, s=s, pb=pb: e.tensor_copy(out=s.ap, in_=pb.ap), r=pb.keys, w=s.keys)
                        else:
                            P.act(lambda e, s=s, pb=pb: e.activation(out=s.ap, in_=pb.ap, func=func), r=pb.keys, w=s.keys)
                        P.dma('sp', lambda e, s=s, tt=tt: e.dma_start(out=dst[tt * 128:(tt + 1) * 128, :], in_=s.ap), r=s.keys, w=[(id(dst), tt * 128)])

                def wcols(c0, n):
                    return [(w_in[:, c0:c0 + n], 0, n)]

                if not A:
                    fm_to_scratch(loadw(wcols(OFF_GLA, 512)), 4, scr["qTg"], scale=QS)
                fm_to_scratch(loadw(wcols(OFF_GLA + 512, 512)), 4, scr["kTg"])
                tm_to_scratch(loadw(wcols(OFF_GLA + 1024, 512)), scr["vg"])
                if not A:
                    tm_to_scratch(loadw(wcols(OFF_GLA + 1536, 512)), scr["gateg"], func=AF.Silu)
                wt = loadw(wcols(OFF_GLA + 2048, 32))
                for tg in TG:
                    lo, n = tg
                    pb = nbank(0, 4)
                    fm(wt, 0, 32, tg, pb)
                    a = stf[sctr[0] % 2]
                    sctr[0] += 1
                    P.act(lambda e, a=a, pb=pb, n=n: e.activation(func=AF.Identity, out=a.ap[0:32, 0:n], in_=pb.ap[0:32, 0:n]), r=pb.keys, w=a.keys)
                    P.dma('sp', lambda e, a=a, lo=lo, n=n: e.dma_start(out=zT[:, lo:lo + n], in_=a.ap[0:32, 0:n]), r=a.keys, w=[("zT", lo)])
                if not A:
                    fm_to_scratch(loadw(wcols(OFF_ML, 512)), 4, scr["qTm"], conv=mcw.ap[:, 0:4, :])
                fm_to_scratch(loadw(wcols(OFF_ML + 512, 512)), 4, scr["kTm"], scale=QS, conv=mcw.ap[:, 4:8, :])
                tm_to_scratch(loadw(wcols(OFF_ML + 1024, 512)), scr["vm"])
                if not A:
                    tm_to_scratch(loadw(wcols(OFF_ML + 1536, 512)), scr["gatem"], func=AF.Sigmoid)
                wt = loadw(wcols(OFF_ML + 2048, 16))
                for tt in range(NT):
                    pb = nbank(0, 4)
                    tm(wt, 0, 16, tt, pb)
                    a = stf[sctr[0] % 2]
                    sctr[0] += 1
                    P.act(lambda e, a=a, pb=pb: e.activation(func=AF.Identity, out=a.ap[:, 0:16], in_=pb.ap[:, 0:16]), r=pb.keys, w=a.keys)
                    P.dma('sp', lambda e, a=a, tt=tt: e.dma_start(out=gatesd[tt * 128:(tt + 1) * 128, :], in_=a.ap[:, 0:16]), r=a.keys, w=[("gatesd", tt)])
                if A:
                    return
                scw = ar.alloc([4, 3], F32)
                for j in range(3):
                    P.dma('sp', lambda e, j=j: e.dma_start(out=scw.ap[:, :, j], in_=sc_conv[j:j + 1, :].rearrange("r (b p) -> p (r b)", p=128), allow_slow_non_contiguous=True), w=scw.keys)
                for cb in range(4):
                    wt = loadw([(w_in[:, OFF_SC + k * 512 + cb * 128: OFF_SC + k * 512 + cb * 128 + 128], k * 128, 128) for k in range(3)])
                    for tg in TG:
                        lo, n = tg
                        pbs = [nbank(0, 4) for _ in range(3)]
                        for k in range(3):
                            fm(wt, k * 128, 128, tg, pbs[k])
                        a = stf[sctr[0] % 2]
                        c = cvf[sctr[0] % 2]
                        s = nstg()
                        P.act(lambda e, a=a, n=n, pb=pbs[1]: e.activation(func=AF.Identity, out=a.ap[:, 0:n], in_=pb.ap[:, 0:n]), r=pbs[1].keys, w=a.keys)
                        P.dve(lambda e, a=a, n=n, pb=pbs[2]: e.tensor_tensor(out=a.ap[:, 0:n], in0=a.ap[:, 0:n], in1=pb.ap[:, 0:n], op=ALU.mult),
                              r=a.keys + pbs[2].keys, w=a.keys)
                        for fn in conv3(a.ap, c.ap, n, scw.ap[:, cb, :], lo):
                            P.dve(fn, r=a.keys + c.keys + scw.keys, w=c.keys)
                        P.dve(lambda e, s=s, c=c, n=n, pb=pbs[0]: e.tensor_tensor(out=s.ap[:, 0:n], in0=c.ap[:, 0:n], in1=pb.ap[:, 0:n], op=ALU.mult),
                              r=c.keys + pbs[0].keys, w=s.keys)
                        P.dma('sp', lambda e, s=s, cb=cb, lo=lo, n=n: e.dma_start(out=yT[8 + cb, :, lo:lo + n], in_=s.ap[:, 0:n]), r=s.keys, w=[("yT", 8 + cb, lo)])
                wsf = ar.alloc([4, 128], F32)
                wsT = ar.alloc([4, 128], BF16)
                bsT = ar.alloc([4], F32)
                cmn = ar.alloc([512], F32)
                P.dma('sp', lambda e: e.dma_start(out=wsf.ap, in_=cm_ws.rearrange("g t s -> t g s")), w=wsf.keys)
                P.dma('sp', lambda e: e.dma_start(out=bsT.ap, in_=cm_bs.rearrange("g t -> t g"), allow_slow_non_contiguous=True), w=bsT.keys)
                P.dma('sp', lambda e: e.dma_start(out=cmn.ap, in_=cm_norm.partition_broadcast(128)), w=cmn.keys)
                pbw = bank(4)
                for g in range(4):
                    P.pe(lambda e, g=g: e.matmul(pbw.ap[:, g * 128:(g + 1) * 128], lhsT=wsf.ap[:, g, :], rhs=identf.ap, start=True, stop=True),
                         r=wsf.keys + identf.keys, w=pbw.keys)
                P.dve(lambda e: e.tensor_copy(out=wsT.ap.rearrange("p g t -> p (g t)"), in_=pbw.ap), r=pbw.keys, w=wsT.keys)
                wv = loadw(wcols(OFF_CM + 512, 512))
                wu = loadw(wcols(OFF_CM, 512))
                gv = ar.alloc([512], F32)
                gu = ar.alloc([512], F32)
                t5 = ar.alloc([512], F32)
                vn = ar.alloc([512], BF16)
                ycm = ar.alloc([512], BF16)
                yt4 = ar.alloc([4, 128], BF16)
                s4 = ar.alloc([4], F32)
                GC = 2.0 * math.sqrt(2.0 / math.pi)

                def gelu(dst, pb):
                    P.act(lambda e: e.activation(out=t5.ap, in_=pb.ap, func=AF.Square), r=pb.keys, w=t5.keys)
                    P.dve(lambda e: e.tensor_scalar(out=t5.ap, in0=t5.ap, scalar1=0.044715, scalar2=1.0, op0=ALU.mult, op1=ALU.add), r=t5.keys, w=t5.keys)
                    P.dve(lambda e: e.tensor_tensor(out=t5.ap, in0=t5.ap, in1=pb.ap, op=ALU.mult), r=t5.keys + pb.keys, w=t5.keys)
                    P.act(lambda e: e.activation(out=t5.ap, in_=t5.ap, func=AF.Sigmoid, scale=GC), r=t5.keys, w=t5.keys)
                    P.dve(lambda e: e.tensor_tensor(out=dst.ap, in0=t5.ap, in1=pb.ap, op=ALU.mult), r=t5.keys + pb.keys, w=dst.keys)

                for tt in range(NT):
                    pv = bank(0 + 2 * (tt % 2))
                    pu = bank(1 + 2 * (tt % 2))
                    tm(wv, 0, 512, tt, pv)
                    tm(wu, 0, 512, tt, pu)
                    gelu(gv, pv)
                    gelu(gu, pu)
                    P.dve(lambda e: e.memset(s4.ap, 0.0), w=s4.keys)
                    P.act(lambda e: e.activation(out=t5.ap, in_=gv.ap, func=AF.Square, accum_out=s4.ap[:, 0:1]), r=gv.keys + s4.keys, w=t5.keys + s4.keys)
                    P.act(lambda e: e.activation(out=s4.ap[:, 1:2], in_=s4.ap[:, 0:1], func=AF.Sqrt, scale=1.0 / 512, bias=EPS), r=s4.keys, w=s4.keys)
                    P.dve(lambda e: e.reciprocal(out=s4.ap[:, 2:3], in_=s4.ap[:, 1:2]), r=s4.keys, w=s4.keys)
                    P.dve(lambda e: e.scalar_tensor_tensor(out=vn.ap, in0=gv.ap, scalar=s4.ap[:, 2:3], in1=cmn.ap, op0=ALU.mult, op1=ALU.mult),
                          r=gv.keys + s4.keys + cmn.keys, w=vn.keys)
                    pS = bank(4 + tt % 2)
                    for g in range(4):
                        P.pe(lambda e, g=g, pS=pS: e.matmul(pS.ap[:, g * 128:(g + 1) * 128], lhsT=wsT.ap[:, g, :], rhs=vn.ap[:, g * 128:(g + 1) * 128], start=True, stop=True),
                             r=wsT.keys + vn.keys, w=pS.keys)
                    for g in range(4):
                        P.dve(lambda e, g=g, pS=pS: e.scalar_tensor_tensor(out=ycm.ap[:, g * 128:(g + 1) * 128], in0=pS.ap[:, g * 128:(g + 1) * 128],
                                                                             scalar=bsT.ap[:, g:g + 1], in1=gu.ap[:, g * 128:(g + 1) * 128], op0=ALU.add, op1=ALU.mult),
                              r=pS.keys + bsT.keys + gu.keys, w=ycm.keys)
                    pT = bank_bf(6 + tt % 2)
                    for g in range(4):
                        P.pe(lambda e, g=g, pT=pT: e.transpose(pT.ap[:, g * 128:(g + 1) * 128], ycm.ap[:, g * 128:(g + 1) * 128], ident.ap),
                             r=ycm.keys + ident.keys, w=pT.keys)
                    P.act(lambda e, pT=pT: e.activation(func=AF.Identity, out=yt4.ap.rearrange("p g t -> p (g t)"), in_=pT.ap[:, 0:512]), r=pT.keys, w=yt4.keys)
                    P.dma('sp', lambda e, tt=tt: e.dma_start(out=yT[12:16, :, tt * 128:(tt + 1) * 128].rearrange("k p t -> p k t"), in_=yt4.ap),
                          r=yt4.keys, w=[("yT", 12, tt)])

            def scans(summary):
                for mi, m in enumerate(("g", "m")):
                    scan_mixer(mi, m, summary)

            def scan_mixer(mi, m, summary):
                A = summary
                if True:
                    ml = m == "m"
                    ar.reset()
                    S = ar.alloc([4, 129], F32)
                    Sbf = ar.alloc([4, 129], BF16)
                    bt = ar.alloc([4], F32)
                    kcs = [ar.alloc([4, 128], BF16) for _ in range(2)]
                    vcs = [ar.alloc([4, 129], BF16) for _ in range(2)]
                    for v in vcs:
                        P.dve(lambda e, v=v: e.memset(v.ap[:, :, 128:129], 1.0), w=v.keys)
                    if ml:
                        gcs = [ar.alloc([16], F32) for _ in range(2)]
                        gs = ar.alloc([16], F32)
                        t4 = ar.alloc([8], F32)
                        Ibc = ar.alloc([4, 128], F32)
                    else:
                        zcs = [ar.alloc([128], F32) for _ in range(2)]
                        for z in zcs:
                            P.dve(lambda e, z=z: e.memset(z.ap[32:33, :], 1.0), w=z.keys)
                        t1 = ar.alloc([512], F32)
                    Lp = ar.alloc([4, 128], F32)
                    ek = ar.alloc([4, 128], F32)
                    el = ar.alloc([4], F32)
                    knT = ar.alloc([4, 128], BF16)
                    kn = ar.alloc([4, 128], BF16)
                    if not A:
                        qcs = [ar.alloc([4, 128], BF16) for _ in range(2)]
                        gts = [ar.alloc([512], BF16) for _ in range(2)]
                        oas = [ar.alloc([4, 129], F32) for _ in range(2)]
                        eq = ar.alloc([4, 128], F32)
                        qi = ar.alloc([4, 128], BF16)
                        am = ar.alloc([4, 128], BF16)
                        ot = ar.alloc([4, 129], F32)
                        hv = ar.alloc([4, 128], F32)
                        s8 = ar.alloc([16], F32)
                        yb = ar.alloc([512], BF16)
                        yt4 = ar.alloc([4, 128], BF16)
                        gnb = ar.alloc([4, 128], F32)
                        P.dma('sp', lambda e, gnb=gnb, ml=ml: e.dma_start(out=gnb.ap.rearrange("p h v -> p (h v)"),
                                                                          in_=(mlstm_norm if ml else gla_norm).partition_broadcast(128)), w=gnb.keys)
                        elu = None
                    kTd, vd = scr["kT" + m], scr["v" + m]
                    uctr = [0]

                    def unit(c, dirn, full, second):
                        i2 = uctr[0] % 2
                        uctr[0] += 1
                        tok = c * 128
                        last = 127 if dirn == 0 else 0
                        Uq, Uk = U[(m, dirn)]
                        kc_t, vc_t = kcs[i2], vcs[i2]
                        P.dma('sp', lambda e: e.dma_start(out=kc_t.ap, in_=kTd[:, :, tok:tok + 128]), r=[(id(kTd), TGLO(tok))], w=kc_t.keys)
                        P.dma('sp', lambda e: e.dma_start(out=vc_t.ap[:, :, 0:128], in_=vd[tok:tok + 128, :].rearrange("t (h d) -> t h d", h=4)),
                              r=[(id(vd), tok)], w=vc_t.keys)
                        if full:
                            qc_t = qcs[i2]
                            qTd = scr["qT" + m]
                            P.dma('sp', lambda e: e.dma_start(out=qc_t.ap, in_=qTd[:, :, tok:tok + 128]), r=[(id(qTd), TGLO(tok))], w=qc_t.keys)
                            if second:
                                gt_t, oa_t = gts[i2], oas[i2]
                                gd, od = scr["gate" + m], scr["oacc" + m]
                                P.dma('sp', lambda e: e.dma_start(out=gt_t.ap, in_=gd[tok:tok + 128, :]), r=[(id(gd), tok)], w=gt_t.keys)
                                P.dma('sp', lambda e: e.dma_start(out=oa_t.ap.rearrange("p h v -> p (h v)"), in_=od[tok:tok + 128, :]), r=[(id(od), tok)], w=oa_t.keys)
                        pX, pY = bank(0), bank(1)
                        if not ml:
                            zc_t = zcs[i2]
                            P.dma('sp', lambda e: e.dma_start(out=zc_t.ap[0:32, :], in_=zT[:, tok:tok + 128]), r=[("zT", TGLO(tok))], w=zc_t.keys)
                            pL = bank(2)
                            P.pe(lambda e: e.matmul(pL.ap, lhsT=zc_t.ap[0:33, :], rhs=Wa[dirn].ap[0:33, :], start=True, stop=True),
                                 r=zc_t.keys + Wa[dirn].keys, w=pL.keys)
                            P.act(lambda e: e.activation(out=t1.ap, in_=pL.ap, func=AF.Exp, scale=-1.0), r=pL.keys, w=t1.keys)
                            P.act(lambda e: e.activation(out=Lp.ap.rearrange("p h d -> p (h d)"), in_=t1.ap, func=AF.Ln, bias=1.0), r=t1.keys, w=Lp.keys)
                        else:
                            gc_t = gcs[i2]
                            P.dma('sp', lambda e: e.dma_start(out=gc_t.ap, in_=gatesd[tok:tok + 128, :]), r=[("gatesd", c)], w=gc_t.keys)
                            P.dve(lambda e: e.tensor_tensor(out=gs.ap, in0=gc_t.ap, in1=gb.ap, op=ALU.add), r=gc_t.keys + gb.keys, w=gs.keys)
                            P.act(lambda e: e.activation(out=t4.ap[:, 0:4], in_=gs.ap[:, dirn * 8 + 4:dirn * 8 + 8], func=AF.Exp, scale=-1.0), r=gs.keys, w=t4.keys)
                            P.act(lambda e: e.activation(out=t4.ap[:, 4:8], in_=t4.ap[:, 0:4], func=AF.Ln, bias=1.0), r=t4.keys, w=t4.keys)
                            P.dve(lambda e: e.tensor_copy(out=Lp.ap, in_=t4.ap[:, 4:8].unsqueeze(2).to_broadcast([128, 4, 128])), r=t4.keys, w=Lp.keys)
                            P.dve(lambda e: e.tensor_copy(out=Ibc.ap, in_=gs.ap[:, dirn * 8:dirn * 8 + 4].unsqueeze(2).to_broadcast([128, 4, 128])), r=gs.keys, w=Ibc.keys)
                        for h in range(4):
                            P.pe(lambda e, h=h: e.matmul(pX.ap[:, h * 128:(h + 1) * 128], lhsT=Lp.ap[:, h, :], rhs=Uq.ap, start=True, stop=True),
                                 r=Lp.keys + Uq.keys, w=pX.keys)
                        for h in range(4):
                            P.pe(lambda e, h=h: e.matmul(pY.ap[:, h * 128:(h + 1) * 128], lhsT=Lp.ap[:, h, :], rhs=Uk.ap, start=True, stop=not ml),
                                 r=Lp.keys + Uk.keys, w=pY.keys)
                            if ml:
                                P.pe(lambda e, h=h: e.matmul(pY.ap[:, h * 128:(h + 1) * 128], lhsT=Ibc.ap[:, h, :], rhs=identf.ap, start=False, stop=True),
                                     r=Ibc.keys + identf.keys, w=pY.keys)
                        pX3 = pX.ap.rearrange("p (h t) -> p h t", h=4)
                        if full:
                            P.act(lambda e: e.activation(out=eq.ap.rearrange("p h t -> p (h t)"), in_=pX.ap, func=AF.Exp), r=pX.keys, w=eq.keys)
                        P.act(lambda e: e.activation(out=el.ap, in_=pX3[:, :, last], func=AF.Exp), r=pX.keys, w=el.keys)
                        P.act(lambda e: e.activation(out=ek.ap.rearrange("p h t -> p (h t)"), in_=pY.ap, func=AF.Exp), r=pY.keys, w=ek.keys)
                        if summary:
                            P.dve(lambda e: e.tensor_tensor(out=bt.ap, in0=bt.ap, in1=pX3[:, :, last], op=ALU.add), r=bt.keys + pX.keys, w=bt.keys)
                        P.dve(lambda e: e.tensor_tensor(out=knT.ap, in0=kc_t.ap, in1=ek.ap, op=ALU.mult), r=kc_t.keys + ek.keys, w=knT.keys)
                        if full:
                            P.dve(lambda e: e.tensor_tensor(out=qi.ap, in0=qc_t.ap, in1=eq.ap, op=ALU.mult), r=qc_t.keys + eq.keys, w=qi.keys)
                        pW = bank_bf(4)
                        for h in range(4):
                            P.pe(lambda e, h=h: e.transpose(pW.ap[:, h * 128:(h + 1) * 128], knT.ap[:, h, :], ident.ap), r=knT.keys + ident.keys, w=pW.keys)
                        P.act(lambda e: e.activation(func=AF.Identity, out=kn.ap.rearrange("p h d -> p (h d)"), in_=pW.ap[:, 0:512]), r=pW.keys, w=kn.keys)
                        if full:
                            pZ = bank(3)
                            for h in range(4):
                                P.pe(lambda e, h=h: e.matmul(pZ.ap[:, h * 128:(h + 1) * 128], lhsT=knT.ap[:, h, :], rhs=qi.ap[:, h, :], start=True, stop=True),
                                     r=knT.keys + qi.keys, w=pZ.keys)
                            P.dve(lambda e: e.tensor_tensor(out=am.ap.rearrange("p h t -> p (h t)"), in0=pZ.ap, in1=masks[dirn].ap.rearrange("p h t -> p (h t)"), op=ALU.mult),
                                  r=pZ.keys + masks[dirn].keys, w=am.keys)
                            pO = [bank(5), bank(6)]
                            for h in range(4):
                                po = pO[h // 2]
                                col = (h % 2) * 129
                                P.pe(lambda e, h=h, po=po, col=col: e.matmul(po.ap[:, col:col + 129], lhsT=am.ap[:, h, :], rhs=vc_t.ap[:, h, :], start=True, stop=False),
                                     r=am.keys + vc_t.keys, w=po.keys)
                                P.pe(lambda e, h=h, po=po, col=col: e.matmul(po.ap[:, col:col + 129], lhsT=qi.ap[:, h, :], rhs=Sbf.ap[:, h, :], start=False, stop=True),
                                     r=qi.keys + Sbf.keys, w=po.keys)
                        pD = [bank(0), bank(1)]
                        for h in range(4):
                            pd = pD[h // 2]
                            col = (h % 2) * 129
                            P.pe(lambda e, h=h, pd=pd, col=col: e.matmul(pd.ap[:, col:col + 129], lhsT=kn.ap[:, h, :], rhs=vc_t.ap[:, h, :], start=True, stop=True),
                                 r=kn.keys + vc_t.keys, w=pd.keys)
                        for b2 in range(2):
                            P.dve(lambda e, b2=b2: e.tensor_tensor(out=S.ap[:, 2 * b2:2 * b2 + 2, :], in0=S.ap[:, 2 * b2:2 * b2 + 2, :],
                                                                   in1=pD[b2].ap[:, 0:258].rearrange("p (h v) -> p h v", h=2), op=ALU.add),
                                  r=S.keys + pD[b2].keys, w=S.keys)
                        P.dve(lambda e: e.tensor_tensor(out=S.ap, in0=S.ap, in1=el.ap.unsqueeze(2).to_broadcast([128, 4, 129]), op=ALU.mult), r=S.keys + el.keys, w=S.keys)
                        P.act(lambda e: e.activation(func=AF.Identity, out=Sbf.ap, in_=S.ap), r=S.keys, w=Sbf.keys)
                        if not full:
                            return
                        od = scr["oacc" + m]

                        def copy_po():
                            for b2 in range(2):
                                P.act(lambda e, b2=b2: e.activation(func=AF.Identity, out=ot.ap[:, 2 * b2:2 * b2 + 2, :], in_=pO[b2].ap[:, 0:258].rearrange("p (h v) -> p h v", h=2)),
                                      r=pO[b2].keys, w=ot.keys)

                        def mlnorm(dst, dkeys):
                            P.dve(lambda e: e.scalar_tensor_tensor(out=s8.ap[:, 8:12], in0=ot.ap[:, :, 128], scalar=-1.0, in1=ot.ap[:, :, 128], op0=ALU.mult, op1=ALU.max),
                                  r=ot.keys, w=s8.keys)
                            P.dve(lambda e: e.tensor_scalar_max(out=s8.ap[:, 8:12], in0=s8.ap[:, 8:12], scalar1=1.0), r=s8.keys, w=s8.keys)
                            P.dve(lambda e: e.reciprocal(out=s8.ap[:, 12:16], in_=s8.ap[:, 8:12]), r=s8.keys, w=s8.keys)
                            P.dve(lambda e: e.tensor_tensor(out=dst, in0=ot.ap[:, :, 0:128], in1=s8.ap[:, 12:16].unsqueeze(2).to_broadcast([128, 4, 128]), op=ALU.mult),
                                  r=ot.keys + s8.keys, w=dkeys)

                        if not second:
                            copy_po()
                            if ml:
                                mlnorm(ot.ap[:, :, 0:128], ot.keys)
                            P.dma('pool', lambda e: e.dma_start(out=od[tok:tok + 128, :], in_=ot.ap.rearrange("p h v -> p (h v)")), r=ot.keys, w=[(id(od), tok)])
                            return
                        if ml:
                            copy_po()
                            mlnorm(hv.ap, hv.keys)
                            P.dve(lambda e: e.tensor_tensor(out=hv.ap, in0=hv.ap, in1=oa_t.ap[:, :, 0:128], op=ALU.add), r=hv.keys + oa_t.keys, w=hv.keys)
                            src = hv.ap
                        else:
                            for b2 in range(2):
                                P.dve(lambda e, b2=b2: e.tensor_tensor(out=ot.ap[:, 2 * b2:2 * b2 + 2, :], in0=oa_t.ap[:, 2 * b2:2 * b2 + 2, :],
                                                                       in1=pO[b2].ap[:, 0:258].rearrange("p (h v) -> p h v", h=2), op=ALU.add),
                                      r=oa_t.keys + pO[b2].keys, w=ot.keys)
                            src = ot.ap[:, :, 0:128]
                        P.dve(lambda e: e.memset(s8.ap[:, 0:4], 0.0), w=s8.keys)
                        for h in range(4):
                            P.act(lambda e, h=h, src=src: e.activation(out=eq.ap[:, h, :], in_=src[:, h, :], func=AF.Square, accum_out=s8.ap[:, h:h + 1]),
                                  r=ot.keys + hv.keys + s8.keys, w=eq.keys + s8.keys)
                        P.act(lambda e: e.activation(out=s8.ap[:, 4:8], in_=s8.ap[:, 0:4], func=AF.Sqrt, scale=1.0 / 128, bias=EPS), r=s8.keys, w=s8.keys)
                        P.dve(lambda e: e.reciprocal(out=s8.ap[:, 0:4], in_=s8.ap[:, 4:8]), r=s8.keys, w=s8.keys)
                        P.dve(lambda e, src=src: e.tensor_tensor(out=hv.ap, in0=src, in1=s8.ap[:, 0:4].unsqueeze(2).to_broadcast([128, 4, 128]), op=ALU.mult),
                              r=ot.keys + hv.keys + s8.keys, w=hv.keys)
                        P.dve(lambda e: e.tensor_tensor(out=hv.ap, in0=hv.ap, in1=gnb.ap, op=ALU.mult), r=hv.keys + gnb.keys, w=hv.keys)
                        P.dve(lambda e: e.tensor_tensor(out=yb.ap, in0=hv.ap.rearrange("p h v -> p (h v)"), in1=gt_t.ap, op=ALU.mult), r=hv.keys + gt_t.keys, w=yb.keys)
                        pT = bank_bf(7)
                        for h in range(4):
                            P.pe(lambda e, h=h: e.transpose(pT.ap[:, h * 128:(h + 1) * 128], yb.ap[:, h * 128:(h + 1) * 128], ident.ap), r=yb.keys + ident.keys, w=pT.keys)
                        P.act(lambda e: e.activation(func=AF.Identity, out=yt4.ap.rearrange("p g t -> p (g t)"), in_=pT.ap[:, 0:512]), r=pT.keys, w=yt4.keys)
                        P.dma('pool', lambda e: e.dma_start(out=yT[4 * mi:4 * mi + 4, :, tok:tok + 128].rearrange("k p t -> p k t"), in_=yt4.ap),
                              r=yt4.keys, w=[("yT", 4 * mi, c)])

                    def zeroS():
                        P.dve(lambda e: e.memset(S.ap, 0.0), w=S.keys)
                        P.dve(lambda e: e.memset(Sbf.ap, 0.0), w=Sbf.keys)

                    for dirn in range(2):
                        second = dirn == 1
                        cchunks = [0, 1] if dirn == 0 else [1, 0]
                        lchunks = list(range(2, NT)) if dirn == 0 else list(range(NT - 1, 1, -1))
                        zeroS()
                        col0 = dirn * 520
                        if summary:
                            P.dve(lambda e: e.memset(bt.ap, 0.0), w=bt.keys)
                            for c in lchunks:
                                unit(c, dirn, False, False)
                            P.dma('sp', lambda e, col0=col0: e.dma_start(out=cc_in[l][mi][:, col0:col0 + 516], in_=S.ap.rearrange("p h v -> p (h v)")),
                                  r=S.keys, w=[("cc_in", l, mi)])
                            P.dma('sp', lambda e, col0=col0: e.dma_start(out=cc_in[l][mi][:, col0 + 516:col0 + 520], in_=bt.ap), r=bt.keys, w=[("cc_in", l, mi)])
                            continue
                        for c in cchunks:
                            unit(c, dirn, True, second)
                        for i in range(4):
                            seg = i if dirn == 0 else 3 - i
                            rows = seg * 128
                            selc = selt.ap[:, dirn * 4 + i:dirn * 4 + i + 1]
                            P.dma('sp', lambda e, rows=rows, col0=col0: e.dma_start(out=ot.ap.rearrange("p h v -> p (h v)"), in_=cc_out[l][mi][rows:rows + 128, col0:col0 + 516]),
                                  r=[("cc_out", l, mi)] + ot.keys, w=ot.keys)
                            P.dma('sp', lambda e, rows=rows, col0=col0: e.dma_start(out=s8.ap[:, 0:4], in_=cc_out[l][mi][rows:rows + 128, col0 + 516:col0 + 520]),
                                  r=[("cc_out", l, mi)] + s8.keys, w=s8.keys)
                            P.act(lambda e, selc=selc: e.activation(out=s8.ap[:, 4:8], in_=s8.ap[:, 0:4], func=AF.Exp, scale=selc), r=s8.keys + selt.keys, w=s8.keys)
                            P.dve(lambda e: e.tensor_tensor(out=S.ap, in0=S.ap, in1=s8.ap[:, 4:8].unsqueeze(2).to_broadcast([128, 4, 129]), op=ALU.mult),
                                  r=S.keys + s8.keys, w=S.keys)
                            P.dve(lambda e, selc=selc: e.scalar_tensor_tensor(out=S.ap, in0=ot.ap, scalar=selc, in1=S.ap, op0=ALU.mult, op1=ALU.add),
                                  r=S.keys + ot.keys + selt.keys, w=S.keys)
                        P.act(lambda e: e.activation(func=AF.Identity, out=Sbf.ap, in_=S.ap), r=S.keys, w=Sbf.keys)
                        for c in lchunks:
                            unit(c, dirn, True, second)

            def outproj():
                ar.reset()
                yts = [ar.alloc([KC, 128], BF16) for _ in range(2)]
                xss = [ar.alloc([512], F32) for _ in range(2)]
                gl = [ar.alloc([512], F32) for _ in range(2)]
                gc = [ar.alloc([512], F32) for _ in range(2)]
                xo = [ar.alloc([512], F32) for _ in range(2)]
                for cb in range(4):
                    wt = loadw([(w_out[:, cb * 512:(cb + 1) * 512], 0, 512)])
                    g_l, g_c = gl[cb % 2], gc[cb % 2]
                    P.dma('sp', lambda e, g_l=g_l, cb=cb: e.dma_start(out=g_l.ap, in_=modrow[0:1, 4096 + cb * 512:4096 + (cb + 1) * 512].partition_broadcast(128)),
                          r=["modrow"], w=g_l.keys)
                    P.dma('sp', lambda e, g_c=g_c, cb=cb: e.dma_start(out=g_c.ap, in_=modrow[1:2, 4096 + cb * 512:4096 + (cb + 1) * 512].partition_broadcast(128)),
                          r=["modrow"], w=g_c.keys)
                    for tt in range(NT):
                        yt, xs, x_o = yts[tt % 2], xss[tt % 2], xo[tt % 2]
                        P.dma('sp', lambda e, yt=yt, tt=tt: e.dma_start(out=yt.ap, in_=yT[:, :, tt * 128:(tt + 1) * 128].rearrange("k p t -> p k t")),
                              r=[("yT", k, TGLO(tt * 128)) for k in range(8, 12)] + [("yT", k, tt) for k in (0, 4, 12)], w=yt.keys)
                        P.dma('sp', lambda e, xs=xs, tt=tt, cb=cb: e.dma_start(out=xs.ap, in_=xsrc[tt * 128:(tt + 1) * 128, cb * 512:(cb + 1) * 512]), r=xkeys(tt), w=xs.keys)
                        pb = nbank(0, 4)
                        for kc in range(KC):
                            P.pe(lambda e, kc=kc, yt=yt, pb=pb, wt=wt: e.matmul(pb.ap, lhsT=yt.ap[:, kc, :], rhs=wt.ap[:, kc, :], start=(kc == 0), stop=(kc == KC - 1)),
                                 r=yt.keys + wt.keys, w=pb.keys)
                        g = g_c if tt < 2 else g_l
                        P.dve(lambda e, x_o=x_o, pb=pb, g=g: e.tensor_tensor(out=x_o.ap, in0=pb.ap, in1=g.ap, op=ALU.mult), r=pb.keys + g.keys, w=x_o.keys)
                        P.dve(lambda e, x_o=x_o, xs=xs: e.tensor_tensor(out=x_o.ap, in0=x_o.ap, in1=xs.ap, op=ALU.add), r=x_o.keys + xs.keys, w=x_o.keys)
                        P.dma('pool', lambda e, x_o=x_o, tt=tt, cb=cb: e.dma_start(out=x1[tt * 128:(tt + 1) * 128, cb * 512:(cb + 1) * 512], in_=x_o.ap),
                              r=x_o.keys, w=[("x1", tt, cb)])

            def ffn():
                mv2 = modvecs(2)

                ar.reset()
                xts = [ar.alloc([D], F32), ar.alloc([D], F32)]
                xns = [ar.alloc([D], BF16), ar.alloc([D], BF16)]
                st4 = [ar.alloc([4], F32), ar.alloc([4], F32)]
                for tt in range(NT):
                    xt, xn, s4 = xts[tt % 2], xns[tt % 2], st4[tt % 2]
                    scale, shift = mv2[1] if tt < 2 else mv2[0]
                    P.dma('sp', lambda e, xt=xt, tt=tt: e.dma_start(out=xt.ap, in_=x1[tt * 128:(tt + 1) * 128, :]), r=[("x1", tt, cb) for cb in range(4)], w=xt.keys)
                    P.dve(lambda e, s4=s4: e.memset(s4.ap, 0.0), w=s4.keys)
                    P.act(lambda e, xt=xt, xn=xn, s4=s4: e.activation(out=xn.ap, in_=xt.ap, func=AF.Square, accum_out=s4.ap[:, 0:1]),
                          r=xt.keys + s4.keys, w=xn.keys + s4.keys)
                    P.act(lambda e, s4=s4: e.activation(out=s4.ap[:, 1:2], in_=s4.ap[:, 0:1], func=AF.Sqrt, scale=1.0 / D, bias=EPS), r=s4.keys, w=s4.keys)
                    P.dve(lambda e, s4=s4: e.reciprocal(out=s4.ap[:, 2:3], in_=s4.ap[:, 1:2]), r=s4.keys, w=s4.keys)
                    P.dve(lambda e, xt=xt, xn=xn, s4=s4: e.tensor_scalar(out=xn.ap, in0=xt.ap, scalar1=s4.ap[:, 2:3], scalar2=None, op0=ALU.mult),
                          r=xt.keys + s4.keys, w=xn.keys)
                    for half in range(2):
                        pb = bank_bf(2 * (tt % 2) + half)
                        for k8 in range(8):
                            kc = half * 8 + k8
                            P.pe(lambda e, pb=pb, xn=xn, kc=kc, k8=k8: e.transpose(pb.ap[:, k8 * 128:(k8 + 1) * 128], xn.ap[:, kc * 128:(kc + 1) * 128], ident.ap),
                                 r=xn.keys + ident.keys, w=pb.keys)
                        for k8 in range(8):
                            kc = half * 8 + k8
                            eng = P.act if k8 % 2 == 0 else P.dve
                            if k8 % 2 == 0:
                                P.act(lambda e, pb=pb, kc=kc, k8=k8, tt=tt, scale=scale, shift=shift: e.activation(
                                    out=big[:, kc, tt * 128:(tt + 1) * 128], in_=pb.ap[:, k8 * 128:(k8 + 1) * 128], func=AF.Identity,
                                    scale=scale.ap[:, kc:kc + 1], bias=shift.ap[:, kc:kc + 1]),
                                    r=pb.keys + scale.keys + shift.keys, w=[("hT", tt)])
                            else:
                                P.dve(lambda e, pb=pb, kc=kc, k8=k8, tt=tt, scale=scale, shift=shift: e.tensor_scalar(
                                    out=big[:, kc, tt * 128:(tt + 1) * 128], in0=pb.ap[:, k8 * 128:(k8 + 1) * 128],
                                    scalar1=scale.ap[:, kc:kc + 1], scalar2=shift.ap[:, kc:kc + 1], op0=ALU.mult, op1=ALU.add),
                                    r=pb.keys + scale.keys + shift.keys, w=[("hT", tt)])
                ar.reset()
                fcw = ar.alloc([NFB, 3], F32)
                for j in range(3):
                    P.dma('sp', lambda e, j=j: e.dma_start(out=fcw.ap[:, :, j], in_=ffn_conv[j:j + 1, :].rearrange("r (b p) -> p (r b)", p=128), allow_slow_non_contiguous=True), w=fcw.keys)
                asb = [ar.alloc([512], F32) for _ in range(2)]
                cvs = [ar.alloc([512], F32) for _ in range(2)]
                ats = [ar.alloc([512], BF16) for _ in range(2)]
                ctr = 0
                for i in range(NFB // 2):
                    wt = loadw([(ffn_up[:, 256 * i:256 * i + 256], 0, 256), (ffn_up[:, DFF + 256 * i:DFF + 256 * i + 256], 256, 256)])
                    for tg in TG:
                        lo, n = tg
                        for j in range(2):
                            fb = 2 * i + j
                            pa = nbank(0, 8)
                            pv = nbank(0, 8)
                            fm(wt, j * 128, 128, tg, pa)
                            fm(wt, 256 + j * 128, 128, tg, pv)
                            a, c, at = asb[ctr % 2], cvs[ctr % 2], ats[ctr % 2]
                            ctr += 1
                            P.act(lambda e, a=a, pa=pa, n=n: e.activation(func=AF.Identity, out=a.ap[:, 0:n], in_=pa.ap[:, 0:n]), r=pa.keys, w=a.keys)
                            for fn in conv3(a.ap, c.ap, n, fcw.ap[:, fb, :], lo):
                                P.dve(fn, r=a.keys + c.keys + fcw.keys, w=c.keys)
                            P.act(lambda e, c=c, n=n: e.activation(out=c.ap[:, 0:n], in_=c.ap[:, 0:n], func=AF.Silu), r=c.keys, w=c.keys)
                            P.dve(lambda e, at=at, c=c, pv=pv, n=n: e.tensor_tensor(out=at.ap[:, 0:n], in0=c.ap[:, 0:n], in1=pv.ap[:, 0:n], op=ALU.mult),
                                  r=c.keys + pv.keys, w=at.keys)
                            P.dma('sp', lambda e, at=at, fb=fb, lo=lo, n=n: e.dma_start(out=actT[fb, :, lo:lo + n], in_=at.ap[:, 0:n]), r=at.keys, w=[("actT", fb, lo)])
                ar.reset()
                bigf = big[:].rearrange("p a b -> p (a b)")
                wds = [V(bigf[:, k * 11264:(k + 1) * 11264].rearrange("p (k n) -> p k n", n=256), [("wd", k)]) for k in range(3)]
                atts = [ar.alloc([NFB, 128], BF16) for _ in range(2)]
                xss = [ar.alloc([256], F32) for _ in range(2)]
                gl = [ar.alloc([256], F32) for _ in range(2)]
                gc = [ar.alloc([256], F32) for _ in range(2)]
                xo = [ar.alloc([256], F32) for _ in range(2)]
                for cb in range(8):
                    wd = wds[cb % 3]
                    for q4 in range(4):
                        P.dma('pool', lambda e, wd=wd, cb=cb, q4=q4: e.dma_start(
                            out=wd.ap[:, 11 * q4:11 * q4 + 11, :],
                            in_=ffn_down[:, cb * 256:(cb + 1) * 256].rearrange("(kc p) n -> p kc n", p=128)[:, 11 * q4:11 * q4 + 11, :]),
                            w=wd.keys + (hkeys if cb < 3 else []))
                    g_l, g_c = gl[cb % 2], gc[cb % 2]
                    P.dma('sp', lambda e, g_l=g_l, cb=cb: e.dma_start(out=g_l.ap, in_=modrow[0:1, 10240 + cb * 256:10240 + (cb + 1) * 256].partition_broadcast(128)),
                          r=["modrow"], w=g_l.keys)
                    P.dma('sp', lambda e, g_c=g_c, cb=cb: e.dma_start(out=g_c.ap, in_=modrow[1:2, 10240 + cb * 256:10240 + (cb + 1) * 256].partition_broadcast(128)),
                          r=["modrow"], w=g_c.keys)
                    for tt in range(NT):
                        att, xs, x_o = atts[tt % 2], xss[tt % 2], xo[tt % 2]
                        P.dma('sp', lambda e, att=att, tt=tt: e.dma_start(out=att.ap, in_=actT[:, :, tt * 128:(tt + 1) * 128].rearrange("k p t -> p k t")),
                              r=[("actT", fb, TGLO(tt * 128)) for fb in range(NFB)], w=att.keys)
                        P.dma('sp', lambda e, xs=xs, tt=tt, cb=cb: e.dma_start(out=xs.ap, in_=x1[tt * 128:(tt + 1) * 128, cb * 256:(cb + 1) * 256]),
                              r=[("x1", tt, cb // 2)], w=xs.keys)
                        pb = nbank(0, 8)
                        for kc in range(NFB):
                            P.pe(lambda e, kc=kc, att=att, pb=pb, wd=wd: e.matmul(pb.ap[:, 0:256], lhsT=att.ap[:, kc, :], rhs=wd.ap[:, kc, :], start=(kc == 0), stop=(kc == NFB - 1)),
                                 r=att.keys + wd.keys, w=pb.keys)
                        g = g_c if tt < 2 else g_l
                        P.dve(lambda e, x_o=x_o, pb=pb, g=g: e.tensor_tensor(out=x_o.ap, in0=pb.ap[:, 0:256], in1=g.ap, op=ALU.mult), r=pb.keys + g.keys, w=x_o.keys)
                        P.dve(lambda e, x_o=x_o, xs=xs: e.tensor_tensor(out=x_o.ap, in0=x_o.ap, in1=xs.ap, op=ALU.add), r=x_o.keys + xs.keys, w=x_o.keys)
                        P.dma('pool', lambda e, x_o=x_o, tt=tt, cb=cb: e.dma_start(out=xdst[tt * 128:(tt + 1) * 128, cb * 256:(cb + 1) * 256], in_=x_o.ap),
                              r=x_o.keys, w=[("xo", l, tt, cb)])
                dm = ar.alloc([4], F32)
                P.dve(lambda e: e.memset(dm.ap, 0.0), r=[("wd", k) for k in range(3)], w=dm.keys + hkeys)

            def finalnorm():
                ar.reset()
                fnb = ar.alloc([D], F32)
                P.dma('sp', lambda e: e.dma_start(out=fnb.ap, in_=final_norm.partition_broadcast(128)), w=fnb.keys)
                xts = [ar.alloc([D], F32), ar.alloc([D], F32)]
                yos = [ar.alloc([D], F32), ar.alloc([D], F32)]
                st4 = [ar.alloc([4], F32), ar.alloc([4], F32)]
                for tt in range(2, NT):
                    xt, yo, s4 = xts[tt % 2], yos[tt % 2], st4[tt % 2]
                    P.dma('sp', lambda e, xt=xt, tt=tt: e.dma_start(out=xt.ap, in_=xdst[tt * 128:(tt + 1) * 128, :]), r=[("xo", l, tt, cb) for cb in range(8)], w=xt.keys)
                    P.dve(lambda e, s4=s4: e.memset(s4.ap, 0.0), w=s4.keys)
                    P.act(lambda e, xt=xt, yo=yo, s4=s4: e.activation(out=yo.ap, in_=xt.ap, func=AF.Square, accum_out=s4.ap[:, 0:1]),
                          r=xt.keys + s4.keys, w=yo.keys + s4.keys)
                    P.act(lambda e, s4=s4: e.activation(out=s4.ap[:, 1:2], in_=s4.ap[:, 0:1], func=AF.Sqrt, scale=1.0 / D, bias=EPS), r=s4.keys, w=s4.keys)
                    P.dve(lambda e, s4=s4: e.reciprocal(out=s4.ap[:, 2:3], in_=s4.ap[:, 1:2]), r=s4.keys, w=s4.keys)
                    P.dve(lambda e, xt=xt, yo=yo, s4=s4: e.scalar_tensor_tensor(out=yo.ap, in0=xt.ap, scalar=s4.ap[:, 2:3], in1=fnb.ap, op0=ALU.mult, op1=ALU.mult),
                          r=xt.keys + s4.keys + fnb.keys, w=yo.keys)
                    P.dma('pool', lambda e, yo=yo, tt=tt: e.dma_start(out=yfinal[(tt - 2) * 128:(tt - 1) * 128, :], in_=yo.ap), r=yo.keys, w=[("yfinal", tt)])


            mv1 = modvecs(1)
            build_hT(xsrc, mv1)
            inproj()
            scans(True)
            for mi_ in range(2):
                P.op('pool', lambda e, mi_=mi_: e.collective_compute("AllGather", ALU.bypass, replica_groups=[[0, 1, 2, 3], [4, 5, 6, 7]],
                                                                     ins=[cc_in[l][mi_]], outs=[cc_out[l][mi_]]),
                     reads=[("cc_in", l, mi_)], writes=[("cc_out", l, mi_)], dma='cc')
            scans(False)
            outproj()
            ffn()
            if l == 1:
                finalnorm()

        for l in range(2):
            emit_layer(l)
        P.finalize(st)
        print("fused", P.stats, flush=True)
    return nc


_CACHE = {}


def _prog():
    if "f" not in _CACHE:
        _CACHE["f"] = build()
    return _CACHE["f"]


def kernel(x, c, ctx, c_ctx, w_mod, b_mod, norm1, norm2, w_in, w_out, gla_wa2, gla_ba, gla_norm,
           mlstm_conv, mlstm_gate_b, mlstm_norm, sc_conv, cm_ws, cm_bs, cm_norm,
           ffn_up, ffn_conv, ffn_down, final_norm):
    f = lambda a: np.ascontiguousarray(np.asarray(a, dtype=np.float32))
    x, c, ctx, c_ctx = f(x), f(c), f(ctx), f(c_ctx)
    Wd = dict(w_mod=w_mod, b_mod=b_mod, norm1=norm1, norm2=norm2, w_in=w_in, w_out=w_out, gla_wa2=gla_wa2, gla_ba=gla_ba,
              gla_norm=gla_norm, mlstm_conv=mlstm_conv, mlstm_gate_b=mlstm_gate_b, mlstm_norm=mlstm_norm, sc_conv=sc_conv,
              cm_ws=cm_ws, cm_bs=cm_bs, cm_norm=cm_norm, ffn_up=ffn_up, ffn_conv=ffn_conv, ffn_down=ffn_down)
    shp = {"b_mod": (1, 12288), "norm1": (1, D), "norm2": (1, D), "gla_norm": (1, 512), "mlstm_gate_b": (1, 16),
           "mlstm_norm": (1, 512), "cm_norm": (1, 512)}
    shared = {"final_norm": f(final_norm).reshape(1, D)}
    for l in range(2):
        for nm, _ in WNAMES:
            a = f(np.asarray(Wd[nm])[l])
            if nm in shp:
                a = a.reshape(shp[nm])
            shared[nm + str(l)] = a
    ins = []
    for core in range(8):
        b, j = core // 4, core % 4
        d = dict(shared)
        d["xin"] = np.concatenate([ctx[b], x[b, j * 2048:(j + 1) * 2048]], axis=0)
        d["c2"] = np.stack([c[b], c_ctx], axis=0)
        s = np.zeros((1, 16), np.float32)
        for i in range(4):
            s[0, i] = 1.0 if i < j else 0.0
            s[0, 4 + i] = 1.0 if (3 - i) > j else 0.0
        d["sel"] = s
        ins.append(d)
    res = run_bass_kernel_spmd(_prog(), ins, core_ids=list(range(8))).results
    out = np.zeros((2, 8192, D), np.float32)
    for core in range(8):
        b, j = core // 4, core % 4
        out[b, j * 2048:(j + 1) * 2048] = res[core]["yfinal"]
    return out
```
